# Optimizing a Trainium2 kernel written in Bass

```python
import jax, jax.numpy as jnp
from jax import lax
import numpy as np

D_MODEL = 1024
BATCH = 4
SEQ = 4096
DEPTH = 2

HEAD_DIM = 64
ROPE_THETA = 10000.0
LN_EPS = 1e-5
A_HEADS = 8
MOBA_BLOCK = 256
MOBA_TOPK = 3
MOBA_Q_CHUNK = 32
B_HEADS = 8
B_KV_HEADS = 2
SWA_WINDOW = 128
C_HEADS = 8
C_NOPE_DIM = 64
C_ROPE_DIM = 32
C_V_DIM = 64
C_KV_LATENT = 128
IDX_HEADS = 8
IDX_DIM = 32
DSA_TOPK = 256
DSA_Q_CHUNK = 128
D_HEADS = 8
SB_Q_BLOCK = 128

MIX_WIDTH = A_HEADS * HEAD_DIM + B_HEADS * HEAD_DIM
EVEN_SIZES = (A_HEADS * HEAD_DIM, A_HEADS * HEAD_DIM, A_HEADS * HEAD_DIM,
              B_HEADS * HEAD_DIM, B_KV_HEADS * HEAD_DIM, B_KV_HEADS * HEAD_DIM, MIX_WIDTH)
ODD_SIZES = (C_HEADS * C_NOPE_DIM, C_HEADS * C_ROPE_DIM, C_KV_LATENT, C_ROPE_DIM,
             IDX_HEADS * IDX_DIM, IDX_DIM, IDX_HEADS,
             D_HEADS * HEAD_DIM, D_HEADS * HEAD_DIM, D_HEADS * HEAD_DIM, MIX_WIDTH)
EVEN_IN = sum(EVEN_SIZES)
ODD_IN = sum(ODD_SIZES)
DEEPNORM_ALPHA = (2 * DEPTH) ** 0.25
DEEPNORM_BETA = (8 * DEPTH) ** -0.25

kernel_name = 'hybrid_moba_swa_dsa_stickbreak_deepnorm'


def _split(x, sizes):
    outs, off = [], 0
    for n in sizes:
        outs.append(x[..., off:off + n])
        off += n
    return outs


def _rope_tables(seq, dim):
    inv = 1.0 / (ROPE_THETA ** (jnp.arange(0, dim, 2, dtype=jnp.float32) / dim))
    ang = jnp.arange(seq, dtype=jnp.float32)[:, None] * inv[None, :]
    return jnp.cos(ang), jnp.sin(ang)


def _apply_rope(x, cos, sin):
    if x.ndim == 4:
        cos, sin = cos[:, None, :], sin[:, None, :]
    cos, sin = cos.astype(x.dtype), sin.astype(x.dtype)
    half = x.shape[-1] // 2
    x1, x2 = x[..., :half], x[..., half:]
    return jnp.concatenate([x1 * cos - x2 * sin, x2 * cos + x1 * sin], axis=-1)


def _layer_norm(x, g, b):
    xf = x.astype(jnp.float32)
    mu = jnp.mean(xf, axis=-1, keepdims=True)
    var = jnp.mean(jnp.square(xf - mu), axis=-1, keepdims=True)
    return ((xf - mu) * lax.rsqrt(var + LN_EPS) * g + b).astype(x.dtype)


def _rms_norm(x, g):
    xf = x.astype(jnp.float32)
    return (xf * lax.rsqrt(jnp.mean(jnp.square(xf), axis=-1, keepdims=True) + LN_EPS) * g).astype(x.dtype)


def _moba_attention(q, k, v):
    bsz, seq, nh, dh = q.shape
    nblk = -(-seq // MOBA_BLOCK)
    pad = nblk * MOBA_BLOCK - seq
    topk = min(MOBA_TOPK, nblk)

    def to_blocks(t):
        t = jnp.pad(t, ((0, 0), (0, pad), (0, 0), (0, 0)))
        return t.reshape(bsz, nblk, MOBA_BLOCK, nh, dh).transpose(0, 3, 1, 2, 4)

    kb, vb = to_blocks(k), to_blocks(v)
    k_mean = jnp.mean(kb.astype(jnp.float32), axis=3).astype(q.dtype)
    n_chunks = seq // MOBA_Q_CHUNK
    qc = q.transpose(0, 2, 1, 3).reshape(bsz, nh, n_chunks, MOBA_Q_CHUNK, dh).transpose(2, 0, 1, 3, 4)
    b_ix = jnp.arange(bsz)[:, None, None, None]
    h_ix = jnp.arange(nh)[None, :, None, None]
    blk_ids = jnp.arange(nblk)
    scale = dh ** -0.5

    def chunk(args):
        qi, ci = args
        t_pos = ci * MOBA_Q_CHUNK + jnp.arange(MOBA_Q_CHUNK)
        own = (ci * MOBA_Q_CHUNK) // MOBA_BLOCK
        gate = jnp.einsum('bhqd,bhnd->bhqn', qi, k_mean).astype(jnp.float32)
        gate = jnp.where(blk_ids < own, gate, -jnp.inf)
        _, sel = lax.top_k(gate, topk)
        sel_ok = sel < own
        k_sel = kb[b_ix, h_ix, sel]
        v_sel = vb[b_ix, h_ix, sel]
        s_sel = jnp.einsum('bhqd,bhqnkd->bhqnk', qi, k_sel).astype(jnp.float32) * scale
        s_sel = jnp.where(sel_ok[..., None], s_sel, -jnp.inf).reshape(bsz, nh, MOBA_Q_CHUNK, topk * MOBA_BLOCK)
        k_own = lax.dynamic_index_in_dim(kb, own, axis=2, keepdims=False)
        v_own = lax.dynamic_index_in_dim(vb, own, axis=2, keepdims=False)
        s_own = jnp.einsum('bhqd,bhkd->bhqk', qi, k_own).astype(jnp.float32) * scale
        k_pos = own * MOBA_BLOCK + jnp.arange(MOBA_BLOCK)
        s_own = jnp.where(k_pos[None, :] <= t_pos[:, None], s_own, -jnp.inf)
        p = jax.nn.softmax(jnp.concatenate([s_sel, s_own], axis=-1), axis=-1).astype(v.dtype)
        p_sel = p[..., :topk * MOBA_BLOCK].reshape(bsz, nh, MOBA_Q_CHUNK, topk, MOBA_BLOCK)
        p_own = p[..., topk * MOBA_BLOCK:]
        return (jnp.einsum('bhqnk,bhqnkd->bhqd', p_sel, v_sel)
                + jnp.einsum('bhqk,bhkd->bhqd', p_own, v_own))

    out = lax.map(chunk, (qc, jnp.arange(n_chunks)))
    return out.transpose(1, 0, 3, 2, 4).reshape(bsz, seq, nh, dh)


def _swa_sink_attention(q, k, v, sinks):
    bsz, seq, nq, dh = q.shape
    nkv = k.shape[2]
    grp = nq // nkv
    w = SWA_WINDOW
    nb = seq // w
    qb = q.reshape(bsz, nb, w, nkv, grp, dh)

    def band(t):
        tb = t.reshape(bsz, nb, w, nkv, dh)
        prev = jnp.pad(tb, ((0, 0), (1, 0), (0, 0), (0, 0), (0, 0)))[:, :-1]
        return jnp.concatenate([prev, tb], axis=2)

    kband, vband = band(k), band(v)
    s = jnp.einsum('bnqkgd,bnskd->bnkgqs', qb, kband).astype(jnp.float32) * dh ** -0.5
    qi = jnp.arange(w)[:, None]
    sj = jnp.arange(2 * w)[None, :]
    in_win = (sj > qi) & (sj <= qi + w)
    real = (jnp.arange(nb) > 0)[:, None, None] | (sj >= w)[None]
    mask = in_win[None] & real
    s = jnp.where(mask[None, :, None, None], s, -jnp.inf)
    sink = jnp.broadcast_to(sinks.astype(jnp.float32).reshape(nkv, grp)[None, None, :, :, None, None],
                            s.shape[:-1] + (1,))
    p = jax.nn.softmax(jnp.concatenate([s, sink], axis=-1), axis=-1)[..., :2 * w].astype(v.dtype)
    o = jnp.einsum('bnkgqs,bnskd->bnqkgd', p, vband)
    return o.reshape(bsz, seq, nq, dh)


def _dsa_attention(q_nope, q_rope, c_kv, k_rope, iq, ik, iw, w_uk, w_uv):
    bsz, seq, nh, _ = q_nope.shape
    n_sel = min(DSA_TOPK, seq // 4)
    nc = seq // DSA_Q_CHUNK
    q_cat = jnp.concatenate([jnp.einsum('bshd,hcd->bshc', q_nope, w_uk), q_rope], axis=-1)
    kv_cat = jnp.concatenate([c_kv, k_rope], axis=-1)
    scale = (C_NOPE_DIM + C_ROPE_DIM) ** -0.5
    idx_scale = (IDX_DIM * IDX_HEADS) ** -0.5
    s_pos = jnp.arange(seq)
    gather = jax.vmap(lambda kv, i: kv[i])

    def chunks(t):
        return t.reshape((bsz, nc, DSA_Q_CHUNK) + t.shape[2:]).swapaxes(0, 1)

    def chunk(args):
        qc, iqc, iwc, ci = args
        t_pos = ci * DSA_Q_CHUNK + jnp.arange(DSA_Q_CHUNK)
        rel = jax.nn.relu(jnp.einsum('bqhd,bsd->bqhs', iqc, ik))
        score = jnp.einsum('bqh,bqhs->bqs', iwc, rel).astype(jnp.float32) * idx_scale
        score = jnp.where(s_pos[None, None, :] <= t_pos[None, :, None], score, -jnp.inf)
        _, sel = lax.top_k(score, n_sel)
        sel_ok = sel <= t_pos[None, :, None]
        kv_sel = gather(kv_cat, sel)
        logits = jnp.einsum('bqhc,bqkc->bqhk', qc, kv_sel).astype(jnp.float32) * scale
        logits = jnp.where(sel_ok[:, :, None, :], logits, -jnp.inf)
        p = jax.nn.softmax(logits, axis=-1).astype(c_kv.dtype)
        return jnp.einsum('bqhk,bqkc->bqhc', p, kv_sel[..., :C_KV_LATENT])

    o_lat = lax.map(chunk, (chunks(q_cat), chunks(iq), chunks(iw), jnp.arange(nc)))
    o_lat = o_lat.swapaxes(0, 1).reshape(bsz, seq, nh, C_KV_LATENT)
    return jnp.einsum('bshc,hcd->bshd', o_lat, w_uv)


def _stick_breaking_attention(q, k, v):
    bsz, seq, nh, dh = q.shape
    nb = seq // SB_Q_BLOCK
    kh = k.transpose(0, 2, 1, 3)
    vh = v.transpose(0, 2, 1, 3)
    qc = q.transpose(0, 2, 1, 3).reshape(bsz, nh, nb, SB_Q_BLOCK, dh).transpose(2, 0, 1, 3, 4)
    s_pos = jnp.arange(seq)

    def block(args):
        qi, bi = args
        t_pos = bi * SB_Q_BLOCK + jnp.arange(SB_Q_BLOCK)
        past = s_pos[None, :] < t_pos[:, None]
        z = jnp.einsum('bhqd,bhsd->bhqs', qi, kh).astype(jnp.float32) * dh ** -0.5
        log_keep = jnp.where(past, jax.nn.log_sigmoid(-z), 0.0)
        log_after = lax.cumsum(log_keep, axis=3, reverse=True) - log_keep
        a = jnp.where(past, jnp.exp(jax.nn.log_sigmoid(z) + log_after), 0.0)
        return jnp.einsum('bhqs,bhsd->bhqd', a.astype(v.dtype), vh)

    out = lax.map(block, (qc, jnp.arange(nb)))
    return out.transpose(1, 0, 3, 2, 4).reshape(bsz, seq, nh, dh)


def _even_mixer(h, w_in, sinks, w_out, rope_h):
    bsz, seq, _ = h.shape
    cos_h, sin_h = rope_h
    aq, ak, av, bq, bk, bv, gate = _split(h @ w_in, EVEN_SIZES)
    aq = _apply_rope(aq.reshape(bsz, seq, A_HEADS, HEAD_DIM), cos_h, sin_h)
    ak = _apply_rope(ak.reshape(bsz, seq, A_HEADS, HEAD_DIM), cos_h, sin_h)
    av = av.reshape(bsz, seq, A_HEADS, HEAD_DIM)
    bq = _apply_rope(bq.reshape(bsz, seq, B_HEADS, HEAD_DIM), cos_h, sin_h)
    bk = _apply_rope(bk.reshape(bsz, seq, B_KV_HEADS, HEAD_DIM), cos_h, sin_h)
    bv = bv.reshape(bsz, seq, B_KV_HEADS, HEAD_DIM)
    oa = _moba_attention(aq, ak, av).reshape(bsz, seq, -1)
    ob = _swa_sink_attention(bq, bk, bv, sinks).reshape(bsz, seq, -1)
    o = jnp.concatenate([oa, ob], axis=-1) * jax.nn.silu(gate)
    return o @ w_out


def _odd_mixer(h, w_in, kv_norm_g, w_uk, w_uv, w_out, rope_r, rope_i):
    bsz, seq, _ = h.shape
    cos_r, sin_r = rope_r
    cos_i, sin_i = rope_i
    cqn, cqr, ckv, ckr, iq, ik, iw, dq, dk, dv, gate = _split(h @ w_in, ODD_SIZES)
    cqn = cqn.reshape(bsz, seq, C_HEADS, C_NOPE_DIM)
    cqr = _apply_rope(cqr.reshape(bsz, seq, C_HEADS, C_ROPE_DIM), cos_r, sin_r)
    ckv = _rms_norm(ckv, kv_norm_g)
    ckr = _apply_rope(ckr, cos_r, sin_r)
    iq = _apply_rope(iq.reshape(bsz, seq, IDX_HEADS, IDX_DIM), cos_i, sin_i)
    ik = _apply_rope(ik, cos_i, sin_i)
    oc = _dsa_attention(cqn, cqr, ckv, ckr, iq, ik, iw, w_uk, w_uv).reshape(bsz, seq, -1)
    od = _stick_breaking_attention(dq.reshape(bsz, seq, D_HEADS, HEAD_DIM),
                                   dk.reshape(bsz, seq, D_HEADS, HEAD_DIM),
                                   dv.reshape(bsz, seq, D_HEADS, HEAD_DIM)).reshape(bsz, seq, -1)
    o = jnp.concatenate([oc, od], axis=-1) * jax.nn.silu(gate)
    return o @ w_out


def setup_inputs(seed: int = 0) -> dict:
    key = jax.random.key(seed)
    ks = jax.random.split(key, 13)
    n_even = (DEPTH + 1) // 2
    n_odd = DEPTH // 2

    def nrm(k, shape, s):
        return jax.random.normal(k, shape, jnp.float32) * s

    return {
        'x': nrm(ks[0], (BATCH, SEQ, D_MODEL), 1.0),
        'c': nrm(ks[1], (BATCH, D_MODEL), 1.0),
        'w_ada': nrm(ks[2], (DEPTH, D_MODEL, 3 * D_MODEL), 0.1 * D_MODEL ** -0.5),
        'b_ada': nrm(ks[3], (DEPTH, 3 * D_MODEL), 0.01),
        'w_in_even': nrm(ks[4], (n_even, D_MODEL, EVEN_IN), D_MODEL ** -0.5),
        'sink_logits': nrm(ks[5], (n_even, B_HEADS), 1.0),
        'w_in_odd': nrm(ks[6], (n_odd, D_MODEL, ODD_IN), D_MODEL ** -0.5),
        'kv_norm_g': 1.0 + nrm(ks[7], (n_odd, C_KV_LATENT), 0.02),
        'w_uk': nrm(ks[8], (n_odd, C_HEADS, C_KV_LATENT, C_NOPE_DIM), C_KV_LATENT ** -0.5),
        'w_uv': nrm(ks[9], (n_odd, C_HEADS, C_KV_LATENT, C_V_DIM), C_KV_LATENT ** -0.5),
        'w_out': nrm(ks[10], (DEPTH, MIX_WIDTH, D_MODEL), MIX_WIDTH ** -0.5 * DEEPNORM_BETA),
        'ln_g': 1.0 + nrm(ks[11], (DEPTH, D_MODEL), 0.02),
        'ln_b': nrm(ks[12], (DEPTH, D_MODEL), 0.02),
    }


def reference(x, c, w_ada, b_ada, w_in_even, sink_logits, w_in_odd, kv_norm_g, w_uk, w_uv, w_out, ln_g, ln_b):
    seq = x.shape[1]
    rope_h = _rope_tables(seq, HEAD_DIM)
    rope_r = _rope_tables(seq, C_ROPE_DIM)
    rope_i = _rope_tables(seq, IDX_DIM)
    cond = jax.nn.silu(c)
    for layer in range(DEPTH):
        shift, scale, gate = jnp.split(cond @ w_ada[layer] + b_ada[layer], 3, axis=-1)
        h = x * (1.0 + scale[:, None, :]) + shift[:, None, :]
        if layer % 2 == 0:
            y = _even_mixer(h, w_in_even[layer // 2], sink_logits[layer // 2], w_out[layer], rope_h)
        else:
            j = layer // 2
            y = _odd_mixer(h, w_in_odd[j], kv_norm_g[j], w_uk[j], w_uv[j], w_out[layer], rope_r, rope_i)
        x = _layer_norm(DEEPNORM_ALPHA * x + (1.0 + gate[:, None, :]) * y, ln_g[layer], ln_b[layer])
    return x
```

```python
from contextlib import ExitStack
import numpy as np
import concourse.bass as bass
import concourse.mybir as mybir
from concourse.bass_utils import run_bass_kernel_spmd

F32 = mybir.dt.float32
BF16 = mybir.dt.bfloat16
AF = mybir.ActivationFunctionType
ALU = mybir.AluOpType
AX = mybir.AxisListType

S = 4096
D = 1024
NT = S // 128
NEG = -30000.0
ALPHA = 4.0 ** 0.25
LN_EPS = 1e-5

ENGS = ("pe", "act", "dve", "pool", "sp")
EP = 24000
NDMA = 20


class Buf:
    __slots__ = ("name", "w", "r", "excl")
    default_r = {}

    def __init__(self, name="", excl=False):
        self.name = name
        self.w = {}
        self.r = dict(Buf.default_r)
        self.excl = excl


def _tok_key(t):
    return t[0] if t[0] not in ("dma", "cc") else (t[0], t[1])


class Prog:
    def __init__(self, nc):
        self.nc = nc
        self.q = {e: [] for e in ENGS}
        self.cnt = {e: 0 for e in ENGS}
        self.seen = {e: {} for e in ENGS}
        self.ndma = 0

    def _deps(self, eng, reads, writes, extra, skip_self):
        deps = {}

        def add(t):
            k = _tok_key(t)
            if skip_self and k == eng:
                return
            if k not in deps or t[1:] >= deps[k][1:]:
                deps[k] = t
        for b in reads:
            for t in b.w.values():
                add(t)
            if b.excl:
                for k_, t in b.r.items():
                    if k_ != eng:
                        add(t)
        for b in writes:
            for t in b.w.values():
                add(t)
            for t in b.r.values():
                add(t)
        for t in extra:
            if t is not None:
                add(t)
        waits = []
        seen = self.seen[eng]
        for k, t in deps.items():
            if k in seen and seen[k][1:] >= t[1:]:
                continue
            seen[k] = t
            waits.append(t)
        return waits

    def _mark(self, tok, reads, writes):
        k = _tok_key(tok)
        for b in reads:
            b.r[k] = tok
        for b in writes:
            b.w = {k: tok}
            b.r = {}

    def op(self, eng, name, *args, reads=(), writes=(), extra=(), skip_self=None, **kw):
        fn = (name, args, kw)
        if skip_self is None:
            skip_self = (eng == "pe")
        waits = self._deps(eng, reads, writes, extra, skip_self)
        c = self.cnt[eng]
        self.cnt[eng] = c + 1
        tok = (eng, c // EP, c % EP + 1)
        self.q[eng].append((waits, fn, tok))
        self._mark(tok, reads, writes)
        return tok

    def dma(self, out, in_, reads=(), writes=(), extra=(), queue="sp"):
        n = self.ndma
        self.ndma += 1
        slot, rnd = n % NDMA, n // NDMA
        tok = ("dma", slot, rnd + 1)
        extra = list(extra)
        if rnd > 0:
            extra.append(("dma", slot, rnd))
        waits = self._deps(queue, reads, writes, extra, False)
        self.q[queue].append((waits, ("dma", out, in_), tok))
        self._mark(tok, reads, writes)
        return tok

    def coll(self, kind, groups, in_ap, out_ap, reads=(), writes=()):
        idx = getattr(self, "ncc", 0)
        self.ncc = idx + 1
        tok = ("cc", idx, 1)
        waits = self._deps("pool", reads, writes, (), False)
        self.q["pool"].append((waits, ("cc", kind, groups, in_ap, out_ap), tok))
        self._mark(tok, reads, writes)
        return tok

    def fence(self):
        toks = {}
        for e in ("pe", "act", "dve", "pool"):
            c = self.cnt[e]
            if c > 0:
                toks[e] = (e, (c - 1) // EP, (c - 1) % EP + 1)
        for slot in range(min(NDMA, self.ndma)):
            last = ((self.ndma - 1 - slot) // NDMA) * NDMA + slot
            toks[("dma", slot)] = ("dma", slot, last // NDMA + 1)
        for s in range(getattr(self, "ncc", 0)):
            toks[("cc", s)] = ("cc", s, 1)
        return toks

    def emit(self, final_waits=()):
        nc = self.nc
        with ExitStack() as es:
            sems = {}
            for e in ENGS:
                nep = (self.cnt[e] + EP - 1) // EP
                for ep in range(max(nep, 1)):
                    sems[(e, ep)] = es.enter_context(nc.semaphore(f"s_{e}_{ep}"))
            for s in range(NDMA):
                sems[("dma", s)] = es.enter_context(nc.semaphore(f"s_dma_{s}"))
            for s in range(getattr(self, "ncc", 0)):
                sems[("cc", s)] = es.enter_context(nc.semaphore(f"s_cc_{s}"))
            block = es.enter_context(nc.Block())

            def wait(engobj, t):
                if t[0] == "cc":
                    engobj.wait_ge(sems[("cc", t[1])], 1)
                elif t[0] == "dma":
                    engobj.wait_ge(sems[("dma", t[1])], 16 * t[2])
                else:
                    engobj.wait_ge(sems[(t[0], t[1])], t[2])

            def mk(e):
                def body(engobj):
                    for waits, fn, tok in self.q[e]:
                        for t in waits:
                            wait(engobj, t)
                        if fn[0] == "cc":
                            _, kind, groups, in_ap, out_ap = fn
                            engobj.collective_compute(kind, ALU.bypass, replica_groups=groups, ins=[in_ap], outs=[out_ap]).then_inc(
                                sems[("cc", tok[1])])
                        elif fn[0] == "dma":
                            _, out, in_ = fn
                            engobj.dma_start(out=out, in_=in_).then_inc(sems[("dma", tok[1])], 16)
                        else:
                            getattr(engobj, fn[0])(*fn[1], **fn[2]).then_inc(sems[(tok[0], tok[1])], 1)
                    if e == "sp":
                        for t in final_waits:
                            wait(engobj, t)
                return body
            block.tensor(mk("pe"))
            block.scalar(mk("act"))
            block.vector(mk("dve"))
            block.gpsimd(mk("pool"))
            block.sync(mk("sp"))


class Ctx:
    def __init__(self, nc, es):
        self.nc = nc
        self.es = es
        self.n = 0

    def sb(self, shape, dt, name=None):
        self.n += 1
        t = self.es.enter_context(self.nc.sbuf_tensor("sb_" + (name or f"t{self.n}"), list(shape), dt))
        return t

    def ps(self, shape, dt, name=None):
        self.n += 1
        t = self.es.enter_context(self.nc.psum_tensor("ps_" + (name or f"p{self.n}"), list(shape), dt))
        return t


class SlabCtx:
    def __init__(self, big, nelem):
        self.big = big
        self.nelem = nelem
        self.off = 0

    def reset(self):
        self.off = 0

    def sb(self, shape, dt, name=None):
        n = 1
        for s_ in shape[1:]:
            n *= s_
        nbf = n * (2 if dt == F32 else 1)
        off = (self.off + 31) // 32 * 32
        assert off + nbf <= self.nelem, f"slab overflow for {name}: {off + nbf} > {self.nelem}"
        v = self.big[0:shape[0], off:off + nbf]
        if dt == F32:
            v = v.bitcast(F32)
        if len(shape) == 3:
            v = v.rearrange("p (a b) -> p a b", b=shape[2])
        self.off = off + nbf
        return v


class Rot:
    def __init__(self, tensors):
        self.t = tensors
        self.b = [Buf() for _ in tensors]
        self.i = -1

    def next(self):
        self.i = (self.i + 1) % len(self.t)
        return self.t[self.i], self.b[self.i]


class NS:
    pass


def setup_common(nc, P, es, dr, extra_consts=(), wbf_cols=1024):
    G = NS()
    cx = Ctx(nc, es)
    G.cx = cx
    G.dr = dr
    G.psF = [cx.ps([128, 512], F32, f"psF{i}") for i in range(6)]
    G.bF = [Buf(excl=True) for _ in range(6)]
    G.psB = [cx.ps([128, 1024], BF16, f"psB{i}") for i in range(2)]
    G.bB = [Buf(excl=True) for _ in range(2)]
    cl = [("ident_f", [128, 128], F32), ("ident_b", [128, 128], BF16), ("one", [65, 1], F32), ("tri", [128, 128], BF16)] + list(extra_consts)
    G.c = {}
    G.bC = Buf()
    for nm, shape, dt in cl:
        t = cx.sb(shape, dt, "c_" + nm)
        sl = tuple(slice(None) for _ in shape)
        P.dma(t[sl], dr[nm][sl], writes=[G.bC])
        G.c[nm] = t
    G.hT = cx.sb([128, 8, S], BF16, "hT"); G.b_hT = Buf()
    G.wst = Rot([cx.sb([128, 512], F32, f"wst{i}") for i in range(2)])
    G.xs = Rot([cx.sb([128, 1024], F32, f"xs{i}") for i in range(2)])
    G.rope = Rot([cx.sb([128, 2, 512], F32, f"rope{i}") for i in range(2)])
    G.wbf = cx.sb([128, 8, wbf_cols], BF16, "wbf"); G.b_wbf = Buf()
    G.osb = Rot([cx.sb([128, 512], F32, f"osb{i}") for i in range(2)])
    G.p_sb = Rot([cx.sb([128, 512], BF16, f"p_sb{i}") for i in range(2)])
    G.pt_sb = Rot([cx.sb([128, 512], BF16, f"pt_sb{i}") for i in range(2)])
    G.rs = Rot([cx.sb([128, 16], F32, f"rs{i}") for i in range(3)])
    G.den = Rot([cx.sb([128, 4], F32, f"den{i}") for i in range(3)])
    G.pipe = Pipe()
    G.ptb_i = [0]
    return G


def emit_adaln(P, G, want_cols=True):
    cx, dr = getattr(G, "lcx", G.cx), G.dr
    cvec = cx.sb([128, 8], F32, "cvec"); cond = cx.sb([128, 8], F32, "cond")
    b_cv, b_cond = Buf(), Buf()
    P.dma(cvec[:, :], dr["cvec"][:, :], writes=[b_cv])
    P.op("act", "activation", out=cond[:, :], in_=cvec[:, :], func=AF.Silu, reads=[b_cv], writes=[b_cond])
    modrow, brow = G.xs.t[0], G.xs.t[1]
    b_modrow, b_brow = G.xs.b[0], G.xs.b[1]
    for r in range(3):
        P.dma(brow[32 * r:32 * r + 1, :], dr["b_ada"][0:1, r * 1024:(r + 1) * 1024], writes=[b_brow])
    wst = G.wst
    if getattr(G, "lcx", None) is not None:
        wst = Rot([cx.sb([128, 512], F32, f"adast{i}") for i in range(8)])
    for j in range(6):
        for k in range(8):
            wt, wb = wst.next()
            P.dma(wt[:, :], dr["w_ada"][k * 128:(k + 1) * 128, j * 512:(j + 1) * 512], writes=[wb])
            P.op("pe", "matmul", G.psF[j][0:1, :], lhsT=cond[:, k:k + 1], rhs=wt[:, :],
                 start=(k == 0), stop=(k == 7), reads=[wb, b_cond], writes=[G.bF[j]])
    for j in range(6):
        r = 32 * (j // 2)
        cs = slice((j % 2) * 512, (j % 2) * 512 + 512)
        if j < 2:
            P.op("dve", "tensor_tensor", out=modrow[r:r + 1, cs], in0=G.psF[j][0:1, :], in1=brow[r:r + 1, cs], op=ALU.add,
                 reads=[G.bF[j], b_brow], writes=[b_modrow])
        else:
            P.op("dve", "scalar_tensor_tensor", out=modrow[r:r + 1, cs], in0=G.psF[j][0:1, :], scalar=1.0, in1=brow[r:r + 1, cs],
                 op0=ALU.add, op1=ALU.add, reads=[G.bF[j], b_brow], writes=[b_modrow])
    G.modrow, G.b_modrow = modrow, b_modrow
    if not want_cols:
        return
    modc = getattr(G, "modc_t", None)
    if modc is None:
        modc = cx.sb([128, 16], F32, "modc")
    b_modc = getattr(G, "b_modc", None) or Buf()
    pc = G.psF[0]
    for j in range(16):
        r = 32 * (j // 8)
        P.op("pe", "matmul", pc[:, j:j + 1], lhsT=modrow[r:r + 1, (j % 8) * 128:(j % 8 + 1) * 128], rhs=G.c["one"][r:r + 1, 0:1],
             start=True, stop=True, reads=[b_modrow, G.bC], writes=[G.bF[0]])
    P.op("dve", "tensor_copy", out=modc[:, :], in_=pc[:, 0:16], reads=[G.bF[0]], writes=[b_modc])
    G.modc, G.b_modc = modc, b_modc


def emit_hT(P, G, x_dram, x_reads=()):
    hT, modc = G.hT, G.modc
    bank = 0
    for ti in range(NT):
        xt, xb = G.xs.next()
        P.dma(xt[:, :], x_dram[ti * 128:(ti + 1) * 128, :], reads=list(x_reads), writes=[xb])
        for half in range(2):
            pt, pb = G.psF[bank], G.bF[bank]
            bank = (bank + 1) % 4
            for _d in range(2):
                P.op("pe", "matmul", G.psF[5][:, 0:256], lhsT=G.c["ident_b"][:, :], rhs=G.c["band"][:, :], start=True, stop=True,
                     reads=[G.bC], writes=[G.bF[5]])
            for j in range(4):
                k = half * 4 + j
                P.op("pe", "transpose", pt[:, j * 128:(j + 1) * 128], xt[:, k * 128:(k + 1) * 128], G.c["ident_f"][:, :],
                     reads=[xb, G.bC], writes=[pb])
            for j in range(4):
                k = half * 4 + j
                if j % 2 == 0:
                    P.op("dve", "tensor_scalar",
                        out=hT[:, k, ti * 128:(ti + 1) * 128], in0=pt[:, j * 128:(j + 1) * 128],
                        scalar1=modc[:, 8 + k:9 + k], scalar2=modc[:, k:k + 1], op0=ALU.mult, op1=ALU.add,
                        reads=[pb, G.b_modc], writes=[G.b_hT])
                else:
                    P.op("act", "activation",
                        out=hT[:, k, ti * 128:(ti + 1) * 128], in_=pt[:, j * 128:(j + 1) * 128],
                        func=AF.Identity, scale=modc[:, 8 + k:9 + k], bias=modc[:, k:k + 1],
                        reads=[pb, G.b_modc], writes=[G.b_hT])


def load_w(P, G, segs, wbf=None, b_wbf=None):
    wbf = G.wbf if wbf is None else wbf
    b_wbf = G.b_wbf if b_wbf is None else b_wbf
    tot = sum(n for _, n in segs)
    assert tot <= wbf.shape[2] and tot <= 512
    wst = getattr(G, "wst_deep", None) or G.wst
    for k in range(8):
        st, sb_ = wst.next()
        o = 0
        for c0, n in segs:
            P.dma(st[:, o:o + n], G.dr["w_in"][k * 128:(k + 1) * 128, c0:c0 + n], writes=[sb_])
            o += n
        P.op("pool", "tensor_copy", out=wbf[:, k, 0:tot], in_=st[:, 0:tot], reads=[sb_], writes=[b_wbf])


def fm_proj(P, G, wc0, M, tc, bank):
    ps, pb = G.psF[bank], G.bF[bank]
    for k in range(8):
        P.op("pe", "matmul", ps[0:M, :], lhsT=G.wbf[:, k, wc0:wc0 + M], rhs=G.hT[:, k, tc * 512:(tc + 1) * 512],
                                           start=(k == 0), stop=(k == 7), reads=[G.b_hT, G.b_wbf], writes=[pb])
    return ps, pb


def fm_proj_rope(P, G, wc_plain, wc_perm, M, dst_fn, b_dst, cosn, sinn, bank0):
    for tc in range(S // 512):
        rp, rb = G.rope.next()
        P.dma(rp[0:M, 0, :], G.dr[cosn][0:M, tc * 512:(tc + 1) * 512], writes=[rb])
        P.dma(rp[0:M, 1, :], G.dr[sinn][0:M, tc * 512:(tc + 1) * 512], writes=[rb])
        ps1, pb1 = fm_proj(P, G, wc_plain, M, tc, bank0 + (tc % 2) * 2)
        ps2, pb2 = fm_proj(P, G, wc_perm, M, tc, bank0 + (tc % 2) * 2 + 1)
        t1, tb1 = G.xs.next()
        P.op("dve", "tensor_tensor", out=t1[0:M, 0:512], in0=ps1[0:M, :], in1=rp[0:M, 0, :], op=ALU.mult,
             reads=[pb1, rb], writes=[tb1])
        P.op("dve", "tensor_tensor", out=t1[0:M, 512:1024], in0=ps2[0:M, :], in1=rp[0:M, 1, :], op=ALU.mult,
             reads=[pb2, rb], writes=[tb1])
        P.op("pool", "tensor_tensor", out=dst_fn(tc), in0=t1[0:M, 0:512], in1=t1[0:M, 512:1024], op=ALU.add,
             reads=[tb1], writes=[b_dst])


def tm_proj(P, G, wbf, b_wbf, wc0, N, ti, bank):
    ps, pb = G.psF[bank], G.bF[bank]
    for k in range(8):
        P.op("pe", "matmul", ps[:, 0:N], lhsT=G.hT[:, k, ti * 128:(ti + 1) * 128], rhs=wbf[:, k, wc0:wc0 + N],
                                           start=(k == 0), stop=(k == 7), reads=[G.b_hT, b_wbf], writes=[pb])
    return ps, pb


def deepen_wst(G, cx, nextra):
    r = Rot(list(G.wst.t) + [cx.sb([128, 512], F32, f"wstx{i}") for i in range(nextra)])
    r.b[0], r.b[1] = G.wst.b[0], G.wst.b[1]
    G.wst_deep = r


class Pipe:
    def __init__(self):
        self.items = []
        self.done = [0, 0, 0]

    def push(self, stages):
        self.items.append(stages)
        n = len(self.items) - 1
        for k in range(3):
            j = n - k
            if j >= 0 and self.done[k] == j:
                self.items[j][k]()
                self.done[k] = j + 1

    def flush(self):
        n = len(self.items)
        for j in range(n):
            for k in range(3):
                if self.done[k] <= j:
                    assert self.done[k] == j
                    self.items[j][k]()
                    self.done[k] = j + 1
        self.items = []
        self.done = [0, 0, 0]


def attn_qtile(P, G, s_parts, s_reads, k0, nkt, addmask_fn, bias_fn, bias_reads, v_fn, v_reads, dv, sbank, obank, scale, blk=4, epi=None):
    pipe = G.pipe
    rs, rsb = G.rs.next()
    P.op("pool", "memset", rs[:, :], 0.0, writes=[rsb])
    op_, opb = G.psF[obank], G.bF[obank]
    den, db = G.den.next()
    col = 0
    chunks = list(range(k0, nkt, 4))
    for ci, c0 in enumerate(chunks):
        c1 = min(c0 + 4, nkt)
        w = (c1 - c0) * 128
        sp_, spb = G.psF[sbank[0]], G.bF[sbank[0]]
        sbank[0], sbank[1] = sbank[1], sbank[0]
        pch, pcb = G.p_sb.next()
        bi = G.ptb_i[0] % 2
        G.ptb_i[0] += 1
        ptp, ptb = G.psB[bi], G.bB[bi]
        pts, ptsb = G.pt_sb.next()
        col0 = col
        nblk = (c1 - c0 + blk - 1) // blk
        col += nblk
        assert col <= 16
        last = (ci == len(chunks) - 1)
        first = (ci == 0)

        def stageA(c0=c0, c1=c1, w=w, sp_=sp_, spb=spb, pch=pch, pcb=pcb, col0=col0):
            masks = addmask_fn(c0, c1)
            for pi, part in enumerate(s_parts):
                lhsT, rhs_fn = part[0], part[1]
                pkw = part[2] if len(part) > 2 else {}
                P.op("pe", "matmul", sp_[:, 0:w], lhsT=lhsT, rhs=rhs_fn(c0 * 128, w), start=(pi == 0),
                     stop=(pi == len(s_parts) - 1 and not masks), reads=s_reads, writes=[spb], **pkw)
            for mi, (lo, mw, rhs_ap, mreads) in enumerate(masks):
                P.op("pe", "matmul", sp_[:, lo:lo + mw], lhsT=G.c["ident_b"][:, :], rhs=rhs_ap, start=False, stop=(mi == len(masks) - 1),
                     reads=[G.bC] + list(mreads), writes=[spb])
            for _d in range(getattr(G, "dummy_mm", 0)):
                P.op("pe", "matmul", G.psF[5][:, 0:256], lhsT=G.c["ident_b"][:, :], rhs=G.c["band"][:, :], start=True, stop=True,
                     reads=[G.bC], writes=[G.bF[5]])
            cc = col0
            for kb0 in range(c0, c1, blk):
                kb1 = min(kb0 + blk, c1)
                lo, hi = (kb0 - c0) * 128, (kb1 - c0) * 128
                b_ap = bias_fn(kb0) if bias_fn is not None else None
                if b_ap is not None:
                    P.op("act", "activation", out=pch[:, lo:hi], in_=sp_[:, lo:hi], func=AF.Exp, scale=scale, bias=b_ap,
                         accum_out=rs[:, cc:cc + 1], reads=[spb] + list(bias_reads), writes=[pcb, rsb], skip_self=True)
                else:
                    P.op("act", "activation", out=pch[:, lo:hi], in_=sp_[:, lo:hi], func=AF.Exp, scale=scale,
                         accum_out=rs[:, cc:cc + 1], reads=[spb], writes=[pcb, rsb], skip_self=True)
                cc += 1

        def stageB(c0=c0, c1=c1, w=w, pch=pch, pcb=pcb, ptp=ptp, ptb=ptb, pts=pts, ptsb=ptsb):
            for kk in range(c1 - c0):
                P.op("pe", "transpose", ptp[:, kk * 128:(kk + 1) * 128], pch[:, kk * 128:(kk + 1) * 128], G.c["ident_b"][:, :],
                     reads=[pcb, G.bC], writes=[ptb])
            if getattr(G, "evac_eng", "dve") == "act":
                P.op("act", "activation", out=pts[:, 0:w], in_=ptp[:, 0:w], func=AF.Identity, reads=[ptb], writes=[ptsb])
            else:
                P.op("dve", "tensor_copy", out=pts[:, 0:w], in_=ptp[:, 0:w], reads=[ptb], writes=[ptsb])

        def stageC(c0=c0, c1=c1, pts=pts, ptsb=ptsb, first=first, last=last):
            for kk in range(c1 - c0):
                kt = c0 + kk
                P.op("pe", "matmul", op_[:, 0:dv], lhsT=pts[:, kk * 128:(kk + 1) * 128], rhs=v_fn(kt), start=(first and kk == 0),
                     stop=(kt == nkt - 1), reads=[ptsb] + list(v_reads), writes=[opb])
            if last:
                P.op("dve", "reduce_sum", out=den[:, 0:1], in_=rs[:, :], axis=AX.X, reads=[rsb], writes=[db])
                if epi is not None:
                    epi(op_, opb, den, db)
        pipe.push([stageA, stageB, stageC])
        if getattr(G, "after_push", None) is not None:
            G.after_push()
    return op_, opb, den, db


EV_COLS = 2624


def _din(nc, dr, name, shape, dt=F32):
    dr[name] = nc.dram_tensor(name, list(shape), dt, kind="ExternalInput").ap()


TOK = 2048


OD_COLS = 2952
DSA_SCALE = 96.0 ** -0.5


def _odd_common(nc, name_extra, wbf_cols):
    dr = {}
    for nm, shp in (("x", [S, D]), ("cvec", [128, 8]), ("w_ada", [D, 3072]), ("b_ada", [1, 3072]), ("w_in", [D, OD_COLS]),
                    ("cos32", [128, S]), ("sin32", [128, S]), ("ident_f", [128, 128]), ("one", [65, 1])):
        _din(nc, dr, nm, shp)
    for nm, shp in (("ident_b", [128, 128]), ("tri", [128, 128]), ("tris", [128, 128])):
        _din(nc, dr, nm, shp, BF16)
    for nm, shp, dt in name_extra:
        _din(nc, dr, nm, shp, dt)
    return dr


def fm_plain_to(P, G, wc0, dst_fn, b_dst):
    for tc in range(S // 512):
        bank = tc % 4
        ps, pb = fm_proj(P, G, wc0, 128, tc, bank)
        if bank % 2 == 0:
            P.op("dve", "tensor_copy", out=dst_fn(tc), in_=ps[:, :], reads=[pb], writes=[b_dst])
        else:
            P.op("act", "activation", out=dst_fn(tc), in_=ps[:, :], func=AF.Identity, reads=[pb], writes=[b_dst])


import ml_dtypes
_BF = ml_dtypes.bfloat16


def _rope_np(seq, dim):
    inv = (1.0 / (np.float32(10000.0) ** (np.arange(0, dim, 2, dtype=np.float32) / np.float32(dim)))).astype(np.float32)
    ang = np.arange(seq, dtype=np.float32)[:, None] * inv[None, :]
    return np.cos(ang).astype(np.float32), np.sin(ang).astype(np.float32)


def _rope_tables_T(seq, dim, reps):
    cos, sin = _rope_np(seq, dim)
    c2 = np.concatenate([cos, cos], axis=1).T
    s2 = np.concatenate([-sin, sin], axis=1).T
    return np.ascontiguousarray(np.tile(c2, (reps, 1))), np.ascontiguousarray(np.tile(s2, (reps, 1)))


def _perm_heads(w, nh, dh):
    w = w.reshape(w.shape[0], nh, dh)
    return np.concatenate([w[:, :, dh // 2:], w[:, :, :dh // 2]], axis=2).reshape(w.shape[0], nh * dh)


def _consts():
    q = np.arange(128)[:, None]
    k = np.arange(128)[None, :]
    negm = NEG * 8.0
    tri = np.where(k <= q, 0.0, negm).astype(np.float32)
    prev = np.where(k > q, 0.0, negm).astype(np.float32)
    return {
        "ident_f": np.eye(128, dtype=np.float32),
        "ident_b": np.eye(128, dtype=np.float32).astype(_BF),
        "one": np.ones((65, 1), np.float32),
        "tri": tri.astype(_BF),
        "band": np.concatenate([prev, tri], axis=1).astype(_BF),
    }


_NC_CACHE = {}


def _get_nc(name, fn):
    if name not in _NC_CACHE:
        _NC_CACHE[name] = fn()
    return _NC_CACHE[name]


def _even_inputs(x, c, w_ada_l, b_ada_l, w_in, sinks):
    cst = _consts()
    cos2, sin2 = _rope_tables_T(S, 64, 2)
    maps = []
    for core in range(8):
        b, g = core // 2, core % 2
        A = lambda c0: w_in[:, c0 + 256 * g: c0 + 256 * g + 256]
        aq, ak, av, bq = A(0), A(512), A(1024), A(1536)
        bk = w_in[:, 2048 + 64 * g: 2048 + 64 * g + 64]
        bv = w_in[:, 2176 + 64 * g: 2176 + 64 * g + 64]
        ga = w_in[:, 2304 + 256 * g: 2304 + 256 * g + 256]
        gb = w_in[:, 2816 + 256 * g: 2816 + 256 * g + 256]
        bkp = _perm_heads(bk, 1, 64)
        wc = np.concatenate([aq, _perm_heads(aq, 4, 64), ak, _perm_heads(ak, 4, 64), bq, _perm_heads(bq, 4, 64),
                             bk, bk, bkp, bkp, av, bv, ga, gb], axis=1)
        assert wc.shape[1] == EV_COLS
        m = dict(cst)
        m.update({
            "x": np.ascontiguousarray(x[b]), "cvec": np.ascontiguousarray(c[b].reshape(8, 128).T),
            "w_ada": np.ascontiguousarray(w_ada_l), "b_ada": np.ascontiguousarray(b_ada_l.reshape(1, 3072)),
            "w_in": np.ascontiguousarray(wc), "cos2": cos2, "sin2": sin2,
            "sinks": np.ascontiguousarray(np.broadcast_to(sinks[4 * g:4 * g + 4][None, :], (128, 4))).astype(np.float32),
        })
        maps.append(m)
    return maps


def _run(nc, maps):
    res = run_bass_kernel_spmd(nc, maps, core_ids=list(range(8)))
    return res.results


def _odd_inputs(x, c, w_ada_l, b_ada_l, w_in, kv_g, w_uk, w_uv):
    cst = _consts()
    q = np.arange(128)[:, None]; k = np.arange(128)[None, :]
    cst["tris"] = np.where(k < q, 0.0, NEG * 8.0).astype(np.float32).astype(_BF)
    cst["zeros"] = np.zeros((128, 512), np.float32)
    cst["trif"] = np.where(k <= q, 0.0, -1e30).astype(np.float32)
    del cst["band"]
    cos32, sin32 = _rope_tables_T(S, 32, 4)
    maps = []
    for core in range(8):
        b, g = core // 2, core % 2
        H4 = lambda c0, dh: w_in[:, c0 + 4 * dh * g: c0 + 4 * dh * g + 4 * dh]
        cqn = H4(0, 64)
        cqr = H4(512, 32)
        ckv = w_in[:, 768:896]
        ckr = w_in[:, 896:928]
        iq = w_in[:, 928:1184]
        ik = w_in[:, 1184:1216]
        iw = w_in[:, 1216:1224]
        dq, dk, dv = H4(1224, 64), H4(1736, 64), H4(2248, 64)
        gc = w_in[:, 2760 + 256 * g: 2760 + 256 * g + 256]
        gd = w_in[:, 3272 + 256 * g: 3272 + 256 * g + 256]
        ckrp = _perm_heads(ckr, 1, 32)
        ikp = _perm_heads(ik, 1, 32)
        wc = np.concatenate([dq, dk, dv, gd, cqn, cqr, _perm_heads(cqr, 4, 32), ckr, ckr, ckr, ckr, ckrp, ckrp, ckrp, ckrp,
                             iq, _perm_heads(iq, 8, 32), ik, ik, ik, ik, ikp, ikp, ikp, ikp, ckv, iw, gc], axis=1)
        assert wc.shape[1] == OD_COLS, wc.shape
        wukT = np.zeros((128, 2, 128), np.float32)
        wuv = np.zeros((128, 4, 64), np.float32)
        for h in range(4):
            wukT[(h % 2) * 64:(h % 2) * 64 + 64, h // 2, :] = w_uk[4 * g + h].T
            wuv[:, h, :] = w_uv[4 * g + h]
        m = dict(cst)
        m.update({
            "x": np.ascontiguousarray(x[b]), "cvec": np.ascontiguousarray(c[b].reshape(8, 128).T),
            "w_ada": np.ascontiguousarray(w_ada_l), "b_ada": np.ascontiguousarray(b_ada_l.reshape(1, 3072)),
            "w_in": np.ascontiguousarray(wc), "cos32": cos32, "sin32": sin32,
            "gbc": np.ascontiguousarray(np.broadcast_to(kv_g[None, :], (128, 128))).astype(np.float32),
            "wukT": wukT, "wuv": wuv,
        })
        maps.append(m)
    return maps


SLAB_ELEMS = 12 * S


def _new_phase(P, SL, G=None):
    SL.reset()
    if G is not None:
        G.wst_deep = None
    Buf.default_r = P.fence()


def phase_even(P, G, SL, drv, O, b_O, g, colA=None, colB=None):
    colA = 256 * g if colA is None else colA
    colB = 512 + 256 * g if colB is None else colB
    _new_phase(P, SL, G)
    dr_save = G.dr
    G.dr = drv
    cx = SL
    sinks_t = cx.sb([128, 4], F32, "sinks"); b_sk = Buf()
    P.dma(sinks_t[:, :], drv["sinks"][:, :], writes=[b_sk])
    esink = cx.sb([128, 4], F32, "esink"); b_esink = Buf()
    P.op("act", "activation", out=esink[:, :], in_=sinks_t[:, :], func=AF.Exp, reads=[b_sk], writes=[b_esink])
    Q = cx.sb([128, 2, S], BF16, "Q"); b_Q = Buf()
    K = cx.sb([128, 2, S], BF16, "K"); b_K = Buf()
    V = cx.sb([128, NT, 256], BF16, "V"); b_V = Buf()
    SG = cx.sb([128, NT, 256], BF16, "SG"); b_SG = Buf()
    kmf = cx.sb([128, 2, 16], F32, "kmf"); b_kmf = Buf()
    kmb = cx.sb([128, 2, 16], BF16, "kmb"); b_kmb = Buf()
    gsb_r = Rot([cx.sb([128, 16], F32, f"gsb{i}") for i in range(2)])
    top_r = Rot([cx.sb([128, 8], F32, f"top{i}") for i in range(2)])
    bias_r = Rot([cx.sb([128, 16], F32, f"bias{i}") for i in range(2)])
    deepen_wst(G, cx, 6)
    load_w(P, G, [(0, 512)])
    for g2 in range(2):
        fm_proj_rope(P, G, g2 * 128, 256 + g2 * 128, 128, lambda tc, g2=g2: Q[:, g2, tc * 512:(tc + 1) * 512], b_Q, "cos2", "sin2", 0)
    load_w(P, G, [(512, 512)])
    for g2 in range(2):
        fm_proj_rope(P, G, g2 * 128, 256 + g2 * 128, 128, lambda tc, g2=g2: K[:, g2, tc * 512:(tc + 1) * 512], b_K, "cos2", "sin2", 0)
    load_w(P, G, [(1792, 256), (2112, 256)])
    for ti in range(NT):
        ps, pb = tm_proj(P, G, G.wbf, G.b_wbf, 0, 512, ti, ti % 4)
        P.op("dve", "tensor_copy", out=V[:, ti, :], in_=ps[:, 0:256], reads=[pb], writes=[b_V])
        P.op("act", "activation", out=SG[:, ti, :], in_=ps[:, 256:512], func=AF.Silu, reads=[pb], writes=[b_SG])
    for g2 in range(2):
        P.op("dve", "tensor_reduce", out=kmf[:, g2, :], in_=K[:, g2, :].rearrange("p (b k) -> p b k", k=256), axis=AX.X, op=ALU.add,
             reads=[b_K], writes=[b_kmf])
    P.op("dve", "tensor_scalar_mul", out=kmb[:, :, :], in0=kmf[:, :, :], scalar1=1.0 / 256.0, reads=[b_kmf], writes=[b_kmb])
    sbank = [0, 1]
    tri = G.c["tri"]
    G.dummy_mm = 3
    for i in range(NT):
        osb, ob = G.osb.next()
        n = i // 2
        qsl = slice(i * 128, (i + 1) * 128)
        for h in range(4):
            g2 = h // 2
            ph = slice((h % 2) * 64, (h % 2) * 64 + 64)
            bias_t, bb = None, None
            if n > 3:
                P.op("pe", "matmul", G.psF[4][:, 0:16], lhsT=Q[ph, g2, qsl], rhs=kmb[ph, g2, :], start=True, stop=True,
                     reads=[b_Q, b_kmb], writes=[G.bF[4]])
                gsb, gb = gsb_r.next()
                P.op("pool", "memset", gsb[:, :], -1e30, writes=[gb])
                P.op("dve", "tensor_copy", out=gsb[:, 0:n], in_=G.psF[4][:, 0:n], reads=[G.bF[4]], writes=[gb])
                top, tb = top_r.next()
                P.op("dve", "max", out=top[:, :], in_=gsb[:, :], reads=[gb], writes=[tb])
                bias_t, bb = bias_r.next()
                P.op("dve", "tensor_scalar", out=bias_t[:, :], in0=gsb[:, :], scalar1=top[:, 2:3], scalar2=NEG, op0=ALU.is_lt, op1=ALU.mult,
                     reads=[gb, tb], writes=[bb])
            nkt = i + 1

            def addmask(c0, c1, nkt=nkt):
                if c1 == nkt:
                    return [((c1 - c0 - 1) * 128, 128, tri[:, :], [])]
                return []

            def bias_fn(kb0, bias_t=bias_t, n=n):
                j = kb0 // 2
                if bias_t is not None and j < n:
                    return bias_t[:, j:j + 1]
                return None
            def epi(op_, opb, den, db, osb=osb, ob=ob, h=h, i=i):
                P.op("dve", "reciprocal", out=den[:, 1:2], in_=den[:, 0:1], reads=[db], writes=[db])
                P.op("dve", "scalar_tensor_tensor", out=osb[:, h * 64:(h + 1) * 64], in0=op_[:, 0:64], scalar=den[:, 1:2],
                     in1=SG[:, i, h * 64:(h + 1) * 64], op0=ALU.mult, op1=ALU.mult, reads=[opb, db, b_SG], writes=[ob])
                if h == 3:
                    P.dma(O.ap(i, colA), osb[:, 0:256], reads=[ob], writes=[O.buf(i)])
            attn_qtile(
                P, G, [(Q[ph, g2, qsl], lambda c, w, g2=g2, ph=ph: K[ph, g2, c:c + w])], [b_Q, b_K], 0, nkt, addmask,
                bias_fn, [bb] if bb is not None else [], lambda kt, h=h: V[:, kt, h * 64:(h + 1) * 64], [b_V], 64, sbank, 2 + (h % 2), 0.125,
                blk=2, epi=epi)
    G.pipe.flush()
    load_w(P, G, [(1024, 512)])
    for g2 in range(2):
        fm_proj_rope(P, G, g2 * 128, 256 + g2 * 128, 128, lambda tc, g2=g2: Q[:, g2, tc * 512:(tc + 1) * 512], b_Q, "cos2", "sin2", 0)
    load_w(P, G, [(1536, 256)])
    fm_proj_rope(P, G, 0, 128, 128, lambda tc: K[:, 0, tc * 512:(tc + 1) * 512], b_K, "cos2", "sin2", 0)
    load_w(P, G, [(2048, 64), (2368, 256)])
    for ti in range(NT):
        ps, pb = tm_proj(P, G, G.wbf, G.b_wbf, 0, 320, ti, ti % 4)
        P.op("dve", "tensor_copy", out=V[:, ti, 0:64], in_=ps[:, 0:64], reads=[pb], writes=[b_V])
        P.op("act", "activation", out=SG[:, ti, :], in_=ps[:, 64:320], func=AF.Silu, reads=[pb], writes=[b_SG])
    band = G.c["band"]
    for i in range(NT):
        osb, ob = G.osb.next()
        qsl = slice(i * 128, (i + 1) * 128)
        k0 = max(i - 1, 0)
        for h in range(4):
            g2 = h // 2
            ph = slice((h % 2) * 64, (h % 2) * 64 + 64)

            def addmask(c0, c1):
                w = (c1 - c0) * 128
                return [(0, w, band[:, 256 - w:256], [])]
            def epi(op_, opb, den, db, osb=osb, ob=ob, h=h, i=i):
                P.op("dve", "tensor_tensor", out=den[:, 2:3], in0=den[:, 0:1], in1=esink[:, h:h + 1], op=ALU.add, reads=[db, b_esink], writes=[db])
                P.op("dve", "reciprocal", out=den[:, 1:2], in_=den[:, 2:3], reads=[db], writes=[db])
                P.op("dve", "scalar_tensor_tensor", out=osb[:, 256 + h * 64:256 + (h + 1) * 64], in0=op_[:, 0:64], scalar=den[:, 1:2],
                     in1=SG[:, i, h * 64:(h + 1) * 64], op0=ALU.mult, op1=ALU.mult, reads=[opb, db, b_SG], writes=[ob])
                if h == 3:
                    P.dma(O.ap(i, colB), osb[:, 256:512], reads=[ob], writes=[O.buf(i)])
                    O.done(i)
            attn_qtile(
                P, G, [(Q[ph, g2, qsl], lambda c, w, ph=ph: K[ph, 0, c:c + w])], [b_Q, b_K], k0, i + 1, addmask,
                None, [], lambda kt: V[:, kt, 0:64], [b_V], 64, sbank, 2 + (h % 2), 0.125, epi=epi)
    G.pipe.flush()
    G.dummy_mm = 0
    G.dr = dr_save


def phase_sb(P, G, SL, drv, O, b_O, g, col=None):
    col = 512 + 256 * g if col is None else col
    _new_phase(P, SL, G)
    dr_save = G.dr
    G.dr = drv
    cx = SL
    Q = cx.sb([128, 2, S], BF16, "Q"); b_Q = Buf()
    K = cx.sb([128, 2, S], BF16, "K"); b_K = Buf()
    V = cx.sb([128, NT, 256], BF16, "V"); b_V = Buf()
    SG = cx.sb([128, NT, 256], BF16, "SG"); b_SG = Buf()
    Df = cx.sb([128, S + 64], F32, "Df"); b_D = Buf()
    te_r = Rot([cx.sb([128, 512], F32, f"te{i}") for i in range(2)])
    ts_r = Rot([cx.sb([128, 512], F32, f"tsp{i}") for i in range(2)])
    negD_r = Rot([cx.sb([128, 2], F32, f"negD{i}") for i in range(2)])
    deepen_wst(G, cx, 3)
    load_w(P, G, [(0, 512)])
    for g2 in range(2):
        fm_plain_to(P, G, g2 * 128, lambda tc, g2=g2: Q[:, g2, tc * 512:(tc + 1) * 512], b_Q)
    for g2 in range(2):
        fm_plain_to(P, G, 256 + g2 * 128, lambda tc, g2=g2: K[:, g2, tc * 512:(tc + 1) * 512], b_K)
    load_w(P, G, [(512, 512)])
    for ti in range(NT):
        ps, pb = tm_proj(P, G, G.wbf, G.b_wbf, 0, 512, ti, ti % 4)
        P.op("dve", "tensor_copy", out=V[:, ti, :], in_=ps[:, 0:256], reads=[pb], writes=[b_V])
        P.op("act", "activation", out=SG[:, ti, :], in_=ps[:, 256:512], func=AF.Silu, reads=[pb], writes=[b_SG])
    tris = G.c["tris"]
    zeros = G.c["zeros"]
    sb_i = 0
    for i in range(NT):
        osb, ob = G.osb.next()
        qsl = slice(i * 128, (i + 1) * 128)
        nkt = i + 1
        for h in range(4):
            g2 = h // 2
            ph = slice((h % 2) * 64, (h % 2) * 64 + 64)
            P.op("pool", "memset", Df[:, 0:1], 0.0, writes=[b_D])
            for c0 in range(0, nkt, 4):
                c1 = min(c0 + 4, nkt)
                w = (c1 - c0) * 128
                bk = sb_i % 2
                sb_i += 1
                sp_, spb = G.psF[bk], G.bF[bk]
                diag = (c1 == nkt)
                P.op("pe", "matmul", sp_[:, 0:w], lhsT=Q[ph, g2, qsl], rhs=K[ph, g2, c0 * 128:c0 * 128 + w], start=True, stop=not diag,
                     reads=[b_Q, b_K], writes=[spb])
                if diag:
                    P.op("pe", "matmul", sp_[:, w - 128:w], lhsT=G.c["ident_b"][:, :], rhs=tris[:, :], start=False, stop=True,
                         reads=[G.bC], writes=[spb])
                te, teb = te_r.next()
                tsp, tsb = ts_r.next()
                P.op("act", "activation", out=te[:, 0:w], in_=sp_[:, 0:w], func=AF.Exp, scale=0.125, reads=[spb], writes=[teb])
                P.op("act", "activation", out=tsp[:, 0:w], in_=te[:, 0:w], func=AF.Ln, bias=1.0, reads=[teb], writes=[tsb])
                P.op("dve", "tensor_tensor_scan", out=Df[:, c0 * 128 + 1:c0 * 128 + 1 + w], data0=tsp[:, 0:w], data1=zeros[:, 0:w],
                     initial=Df[:, c0 * 128:c0 * 128 + 1], op0=ALU.add, op1=ALU.add, reads=[tsb, b_D, G.bC], writes=[b_D])
                P.op("dve", "scalar_tensor_tensor", out=Df[:, c0 * 128:c0 * 128 + w], in0=sp_[:, 0:w], scalar=0.125,
                     in1=Df[:, c0 * 128:c0 * 128 + w], op0=ALU.mult, op1=ALU.add, reads=[spb, b_D], writes=[b_D])
            negD, nb_ = negD_r.next()
            P.op("dve", "tensor_scalar_mul", out=negD[:, 0:1], in0=Df[:, nkt * 128:nkt * 128 + 1], scalar1=-1.0, reads=[b_D], writes=[nb_])
            op_, opb = G.psF[2 + (h % 2)], G.bF[2 + (h % 2)]
            chunks = list(range(0, nkt, 4))
            for ci, c0 in enumerate(chunks):
                c1 = min(c0 + 4, nkt)
                w = (c1 - c0) * 128
                pch, pcb = G.p_sb.next()
                bi = G.ptb_i[0] % 2
                G.ptb_i[0] += 1
                ptp, ptb = G.psB[bi], G.bB[bi]
                pts, ptsb = G.pt_sb.next()
                last = (ci == len(chunks) - 1)

                def stA(c0=c0, w=w, pch=pch, pcb=pcb, negD=negD, nb_=nb_):
                    P.op("act", "activation", out=pch[:, 0:w], in_=Df[:, c0 * 128:c0 * 128 + w], func=AF.Exp, bias=negD[:, 0:1],
                         reads=[b_D, nb_], writes=[pcb])

                def stB(c0=c0, c1=c1, w=w, pch=pch, pcb=pcb, ptp=ptp, ptb=ptb, pts=pts, ptsb=ptsb):
                    for kk in range(c1 - c0):
                        P.op("pe", "transpose", ptp[:, kk * 128:(kk + 1) * 128], pch[:, kk * 128:(kk + 1) * 128], G.c["ident_b"][:, :],
                             reads=[pcb, G.bC], writes=[ptb])
                    P.op("act", "activation", out=pts[:, 0:w], in_=ptp[:, 0:w], func=AF.Identity, reads=[ptb], writes=[ptsb])

                def stC(c0=c0, c1=c1, pts=pts, ptsb=ptsb, last=last, op_=op_, opb=opb, osb=osb, ob=ob, h=h, i=i, nkt=nkt):
                    for kk in range(c1 - c0):
                        kt = c0 + kk
                        P.op("pe", "matmul", op_[:, 0:64], lhsT=pts[:, kk * 128:(kk + 1) * 128], rhs=V[:, kt, h * 64:(h + 1) * 64],
                             start=(kt == 0), stop=(kt == nkt - 1), reads=[ptsb, b_V], writes=[opb])
                    if last:
                        P.op("dve", "tensor_tensor", out=osb[:, h * 64:(h + 1) * 64], in0=op_[:, 0:64], in1=SG[:, i, h * 64:(h + 1) * 64],
                             op=ALU.mult, reads=[opb, b_SG], writes=[ob])
                        if h == 3:
                            P.dma(O.ap(i, col), osb[:, 0:256], reads=[ob], writes=[O.buf(i)])
                G.pipe.push([stA, stB, stC])
    G.pipe.flush()
    G.dr = dr_save


def phase_dsa(P, G, SL, drv, O, b_O, g, SELB, b_SELB, mode, nbis=14, col=None):
    col = 256 * g if col is None else col
    _new_phase(P, SL, G)
    dr_save = G.dr
    G.dr = drv
    cx = SL
    write = (mode in ("write", "solo"))
    CQN = cx.sb([128, 2, S], BF16, "CQN"); b_CQN = Buf()
    CQR = cx.sb([128, S], BF16, "CQR"); b_CQR = Buf()
    CKR = cx.sb([128, S], BF16, "CKR"); b_CKR = Buf()
    CKVT = cx.sb([128, S], BF16, "CKVT"); b_CKVT = Buf()
    CKV = cx.sb([128, NT, 128], BF16, "CKV"); b_CKV = Buf()
    SG = cx.sb([128, NT, 256], BF16, "SG"); b_SG = Buf()
    if write:
        IQ = cx.sb([128, 2, S], BF16, "IQ"); b_IQ = Buf()
        IK = cx.sb([128, S], BF16, "IK"); b_IK = Buf()
        IW = cx.sb([128, NT, 8], F32, "IW"); b_IW = Buf()
        te_r = G.xs
    else:
        selb_t = [cx.sb([128, S], BF16, f"selb{i}") for i in range(2)]
    st_r = Rot([cx.sb([128, 4], F32, f"rst{i}") for i in range(2)])
    junk = cx.sb([128, 128], F32, "junk"); b_junk = Buf()
    wukT_b = cx.sb([128, 2, 128], BF16, "wukT_b"); wuv_b = cx.sb([128, 4, 64], BF16, "wuv_b"); b_wu = Buf()
    ql_r = Rot([cx.sb([128, 128], BF16, f"ql{i}") for i in range(2)])
    ol_r = Rot([cx.sb([128, 128], BF16, f"ol{i}") for i in range(2)])
    olT_r = Rot([cx.sb([128, 128], BF16, f"olT{i}") for i in range(2)])
    gbc, trif = G.c["gbc"], G.c["trif"]
    st, sb_ = G.wst.next()
    P.dma(st[:, 0:256], drv["wukT"][:, :, :].rearrange("p a b -> p (a b)"), writes=[sb_])
    P.dma(st[:, 256:512], drv["wuv"][:, :, :].rearrange("p a b -> p (a b)"), writes=[sb_])
    P.op("pool", "tensor_copy", out=wukT_b[:, :, :].rearrange("p a b -> p (a b)"), in_=st[:, 0:256], reads=[sb_], writes=[b_wu])
    P.op("pool", "tensor_copy", out=wuv_b[:, :, :].rearrange("p a b -> p (a b)"), in_=st[:, 256:512], reads=[sb_], writes=[b_wu])
    load_w(P, G, [(1024, 256)])
    for g2 in range(2):
        fm_plain_to(P, G, g2 * 128, lambda tc, g2=g2: CQN[:, g2, tc * 512:(tc + 1) * 512], b_CQN)
    load_w(P, G, [(1280, 512)])
    fm_proj_rope(P, G, 0, 128, 128, lambda tc: CQR[:, tc * 512:(tc + 1) * 512], b_CQR, "cos32", "sin32", 0)
    fm_proj_rope(P, G, 256, 384, 128, lambda tc: CKR[:, tc * 512:(tc + 1) * 512], b_CKR, "cos32", "sin32", 0)
    if write:
        load_w(P, G, [(1792, 512)])
        for g2 in range(2):
            fm_proj_rope(P, G, g2 * 128, 256 + g2 * 128, 128, lambda tc, g2=g2: IQ[:, g2, tc * 512:(tc + 1) * 512], b_IQ, "cos32", "sin32", 0)
        load_w(P, G, [(2304, 256)])
        fm_proj_rope(P, G, 0, 128, 128, lambda tc: IK[:, tc * 512:(tc + 1) * 512], b_IK, "cos32", "sin32", 0)
    load_w(P, G, [(2560, 392)])
    for ti in range(NT):
        ps, pb = tm_proj(P, G, G.wbf, G.b_wbf, 0, 392, ti, ti % 4)
        stt, stb = st_r.next()
        P.op("act", "activation", out=junk[:, :], in_=ps[:, 0:128], func=AF.Square, accum_out=stt[:, 0:1], reads=[pb], writes=[b_junk, stb])
        P.op("dve", "tensor_scalar", out=stt[:, 1:2], in0=stt[:, 0:1], scalar1=1.0 / 128.0, scalar2=LN_EPS, op0=ALU.mult, op1=ALU.add,
             reads=[stb], writes=[stb])
        P.op("act", "activation", out=stt[:, 2:3], in_=stt[:, 1:2], func=AF.Sqrt, reads=[stb], writes=[stb])
        P.op("dve", "reciprocal", out=stt[:, 3:4], in_=stt[:, 2:3], reads=[stb], writes=[stb])
        P.op("dve", "scalar_tensor_tensor", out=CKV[:, ti, :], in0=ps[:, 0:128], scalar=stt[:, 3:4], in1=gbc[:, :], op0=ALU.mult, op1=ALU.mult,
             reads=[pb, stb, G.bC], writes=[b_CKV])
        if write:
            P.op("dve", "tensor_copy", out=IW[:, ti, :], in_=ps[:, 128:136], reads=[pb], writes=[b_IW])
        P.op("act", "activation", out=SG[:, ti, :], in_=ps[:, 136:392], func=AF.Silu, reads=[pb], writes=[b_SG])
        bi = ti % 2
        P.op("pe", "transpose", G.psB[bi][:, 0:128], CKV[:, ti, :], G.c["ident_b"][:, :], reads=[b_CKV, G.bC], writes=[G.bB[bi]])
        P.op("dve", "tensor_copy", out=CKVT[:, ti * 128:(ti + 1) * 128], in_=G.psB[bi][:, 0:128], reads=[G.bB[bi]], writes=[b_CKVT])

    carved = []

    def carve():
        nb = Buf()
        for src in (G.b_hT.r, G.b_hT.w):
            for k_, t_ in src.items():
                if k_ not in nb.r or t_[1:] >= nb.r[k_][1:]:
                    nb.r[k_] = t_
        carved.append(nb)
        return nb
    if write:
        selb_t = [G.hT[:, 0, :], G.hT[:, 4, :]]
        cj_t = [G.hT[:, 1, :], G.hT[:, 7, :]]
        sc_t = [G.hT[:, 2:4, :].rearrange("p a s -> p (a s)").bitcast(F32), G.hT[:, 5:7, :].rearrange("p a s -> p (a s)").bitcast(F32)]
        b_selb = [carve(), carve()]
        b_cjs = [carve(), carve()]
        b_scs = [carve(), carve()]
        ck = cx.sb([128, 32], F32, "ck"); b_ck = Buf()
        for k_ in range(nbis + 1):
            P.op("pool", "memset", ck[:, k_:k_ + 1], 0.5 ** (k_ + 1), writes=[b_ck])
        bs_r = Rot([cx.sb([128, 48], F32, f"bs{i}") for i in range(3)])
    else:
        b_selb = [Buf(), Buf()]
    xc = [0]

    def prologue(i):
        nkt = i + 1
        n = nkt * 128
        qsl = slice(i * 128, (i + 1) * 128)
        selbias, bsel = selb_t[i % 2], b_selb[i % 2]
        if not write:
            P.dma(selbias[:, 0:n], SELB[i * 128:(i + 1) * 128, 0:n], reads=[b_SELB], writes=[bsel])
            yield
            return
        score, b_sc = sc_t[i % 2], b_scs[i % 2]
        cjunk, b_cj = cj_t[i % 2], b_cjs[i % 2]
        for c0 in range(0, nkt, 4):
            c1 = min(c0 + 4, nkt)
            w = (c1 - c0) * 128
            cols = slice(c0 * 128, c0 * 128 + w)
            for hi in range(8):
                q4, grp = hi % 4, hi // 4
                rows = slice(32 * q4, 32 * q4 + 32)
                bk = 4 + (xc[0] % 2)
                xc[0] += 1
                kw = {"tile_position": (96, 0)} if q4 == 3 else {}
                P.op("pe", "matmul", G.psF[bk][:, 0:w], lhsT=IQ[rows, grp, qsl], rhs=IK[rows, cols], start=True, stop=True,
                     reads=[b_IQ, b_IK], writes=[G.bF[bk]], **kw)
                te, teb = te_r.next()
                P.op("act", "activation", out=te[:, 0:w], in_=G.psF[bk][:, 0:w], func=AF.Relu, reads=[G.bF[bk]], writes=[teb])
                if hi == 0:
                    P.op("dve", "tensor_scalar_mul", out=score[:, cols], in0=te[:, 0:w], scalar1=IW[:, i, 0:1], reads=[teb, b_IW], writes=[b_sc])
                else:
                    P.op("dve", "scalar_tensor_tensor", out=score[:, cols], in0=te[:, 0:w], scalar=IW[:, i, hi:hi + 1], in1=score[:, cols],
                         op0=ALU.mult, op1=ALU.add, reads=[teb, b_IW, b_sc], writes=[b_sc])
                yield
        bs, bb = bs_r.next()
        LO, HI, MID, CNT, TT, W0 = (bs[:, j:j + 1] for j in range(6))
        WK = bs[:, 16:48]
        if i >= 2:
            P.op("dve", "tensor_reduce", out=LO, in_=score[:, 0:n], axis=AX.X, op=ALU.min, reads=[b_sc], writes=[bb])
            P.op("dve", "tensor_scalar_add", out=LO, in0=LO, scalar1=-1.0, reads=[bb], writes=[bb])
        else:
            P.op("dve", "memset", LO, -1e29, writes=[bb])
        P.op("dve", "tensor_tensor", out=score[:, n - 128:n], in0=score[:, n - 128:n], in1=trif[:, :], op=ALU.add, reads=[b_sc, G.bC, bb], writes=[b_sc])
        yield
        if i >= 2:
            P.op("dve", "max", out=bs[:, 8:16], in_=score[:, 0:n], reads=[b_sc], writes=[bb])
            P.op("dve", "tensor_tensor", out=W0, in0=bs[:, 8:9], in1=LO, op=ALU.subtract, reads=[bb], writes=[bb])
            P.op("dve", "tensor_scalar_mul", out=WK, in0=ck[:, :], scalar1=W0, reads=[bb, b_ck], writes=[bb])
            P.op("dve", "tensor_tensor", out=MID, in0=LO, in1=WK[:, 0:1], op=ALU.add, reads=[bb], writes=[bb])
            yield
            na = max(128, (n * 19 // 32) // 128 * 128)
            bA, bD, bjD = Buf(), Buf(), Buf()
            for src_ in (bb.w, bb.r):
                for k_, t_ in src_.items():
                    bA.r[k_] = t_; bD.r[k_] = t_
            G2, C2 = bs[:, 6:7], bs[:, 7:8]
            for it in range(nbis):
                P.op("act", "activation", out=cjunk[:, 0:na], in_=score[:, 0:na], func=AF.Sign, scale=-1.0, bias=MID, accum_out=CNT,
                     reads=[b_sc, bb], writes=[bA, b_cj])
                P.op("dve", "tensor_scalar", out=cjunk[:, na:n], in0=score[:, na:n], scalar1=MID, scalar2=0.0, op0=ALU.is_gt, op1=ALU.add,
                     accum_out=G2, reads=[b_sc, bb], writes=[bD, bjD])
                P.op("dve", "scalar_tensor_tensor", out=C2, in0=G2, scalar=-2.0, in1=CNT, op0=ALU.mult, op1=ALU.add,
                     reads=[bA, bD], writes=[bb])
                P.op("dve", "scalar_tensor_tensor", out=TT, in0=C2, scalar=float(na - 511), in1=WK[:, it:it + 1], op0=ALU.is_le, op1=ALU.mult,
                     reads=[bb], writes=[bb])
                P.op("dve", "scalar_tensor_tensor", out=MID, in0=TT, scalar=WK[:, it + 1:it + 2], in1=MID, op0=ALU.subtract, op1=ALU.add,
                     reads=[bb, bA, bD], writes=[bb])
                yield
            P.op("dve", "tensor_tensor", out=LO, in0=MID, in1=WK[:, nbis:nbis + 1], op=ALU.subtract, reads=[bb], writes=[bb])
        P.op("dve", "tensor_scalar", out=selbias[:, 0:n], in0=score[:, 0:n], scalar1=LO, scalar2=NEG * 8.0, op0=ALU.is_le, op1=ALU.mult,
             reads=[b_sc, bb], writes=[bsel])
        if mode == "write":
            P.dma(SELB[i * 128:(i + 1) * 128, 0:n], selbias[:, 0:n], reads=[bsel], writes=[b_SELB])
        yield

    def nsteps(i):
        if not write:
            return 1
        return 8 * ((i + 4) // 4) + 3 + (nbis if i >= 2 else 0)

    sbank = [0, 1]
    G.evac_eng = "act"
    DEPTH = 2 if write else 1
    gens = {}
    for j in range(min(DEPTH, NT)):
        gens[j] = prologue(j)
    for _ in gens[0]:
        pass
    for i in range(NT):
        osb, ob = G.osb.next()
        qsl = slice(i * 128, (i + 1) * 128)
        nkt = i + 1
        selbias, bsel = selb_t[i % 2], b_selb[i % 2]
        if i + DEPTH < NT and (i + DEPTH) not in gens:
            gens[i + DEPTH] = prologue(i + DEPTH)
        active = [(j, gens[j]) for j in range(i + 1, i + DEPTH + 1) if j in gens]
        npush = 4 * ((nkt + 3) // 4)
        pers = [(gj, (nsteps(j) + npush * DEPTH - 1) // (npush * DEPTH) if j > i + 1 else (nsteps(j) + npush - 1) // npush) for j, gj in active]

        def after_push(pers=pers):
            for gj, per in pers:
                for _ in range(per):
                    next(gj, None)
        G.after_push = after_push
        for h in range(4):
            g2 = h // 2
            ph = slice((h % 2) * 64, (h % 2) * 64 + 64)
            rows = slice(32 * h, 32 * h + 32)
            kw = {"tile_position": (96, 0)} if h == 3 else {}
            ql, qlb = ql_r.next()
            P.op("pe", "matmul", G.psF[4][:, 0:128], lhsT=wukT_b[ph, g2, :], rhs=CQN[ph, g2, qsl], start=True, stop=True,
                 reads=[b_wu, b_CQN], writes=[G.bF[4]])
            P.op("act", "activation", out=ql[:, :], in_=G.psF[4][:, 0:128], func=AF.Identity, reads=[G.bF[4]], writes=[qlb])

            def addmask(c0, c1, selbias=selbias, bsel=bsel):
                w = (c1 - c0) * 128
                return [(0, w, selbias[:, c0 * 128:c0 * 128 + w], [bsel])]

            def epi(op_, opb, den, db, osb=osb, ob=ob, h=h, i=i):
                P.op("dve", "reciprocal", out=den[:, 1:2], in_=den[:, 0:1], reads=[db], writes=[db])
                ol, olb = ol_r.next()
                P.op("dve", "tensor_scalar_mul", out=ol[:, :], in0=op_[:, 0:128], scalar1=den[:, 1:2], reads=[opb, db], writes=[olb])
                P.op("pe", "transpose", G.psB[0][:, 0:128], ol[:, :], G.c["ident_b"][:, :], reads=[olb, G.bC], writes=[G.bB[0]])
                olT, oltb = olT_r.next()
                P.op("act", "activation", out=olT[:, :], in_=G.psB[0][:, 0:128], func=AF.Identity, reads=[G.bB[0]], writes=[oltb])
                P.op("pe", "matmul", G.psF[5][:, 0:64], lhsT=olT[:, :], rhs=wuv_b[:, h, :], start=True, stop=True, reads=[oltb, b_wu], writes=[G.bF[5]])
                P.op("dve", "tensor_tensor", out=osb[:, h * 64:(h + 1) * 64], in0=G.psF[5][:, 0:64], in1=SG[:, i, h * 64:(h + 1) * 64], op=ALU.mult,
                     reads=[G.bF[5], b_SG], writes=[ob])
                if h == 3:
                    P.dma(O.ap(i, col), osb[:, 0:256], reads=[ob], writes=[O.buf(i)])
                    O.done(i)
            attn_qtile(
                P, G, [(ql[:, :], lambda c, w: CKVT[:, c:c + w]), (CQR[rows, qsl], lambda c, w, rows=rows: CKR[rows, c:c + w], kw)],
                [qlb, b_CKVT, b_CQR, b_CKR], 0, nkt, addmask, None, [], lambda kt: CKV[:, kt, :], [b_CKV], 128, sbank, 2 + (h % 2), DSA_SCALE, epi=epi)
        G.after_push = None
        if (i + 1) in gens:
            for _ in gens[i + 1]:
                pass
    G.pipe.flush()
    G.evac_eng = "dve"
    for cb in carved:
        for src in (cb.r, cb.w):
            for k_, t_ in src.items():
                if k_ not in G.b_hT.r or t_[1:] >= G.b_hT.r[k_][1:]:
                    G.b_hT.r[k_] = t_
    G.dr = dr_save


def phase_out(P, G, SL, drv, x_src, x_reads, O, b_O, dst, b_dst, finals=None, gathered=False):
    _new_phase(P, SL, G)
    cx = SL
    psF, bF = G.psF, G.bF
    rows = cx.sb([1, 3, D], F32, "rows"); b_rows = Buf()
    P.dma(rows[0:1, 0, :], G.grow[0:1, :], reads=[G.b_grow], writes=[b_rows])
    P.dma(rows[0:1, 1, :], drv["lng"][:, :], writes=[b_rows])
    P.dma(rows[0:1, 2, :], drv["lnb"][:, :], writes=[b_rows])
    bc = cx.sb([128, 3, D], F32, "bc"); b_bc = Buf()
    ones_row = G.c["ones_row"]
    for j in range(3):
        for hh in range(2):
            bk = j * 2 + hh
            P.op("pe", "matmul", psF[bk][:, :], lhsT=ones_row[0:1, :], rhs=rows[0:1, j, hh * 512:(hh + 1) * 512], start=True, stop=True,
                 reads=[G.bC, b_rows], writes=[bF[bk]])
            P.op("dve", "tensor_copy", out=bc[:, j, hh * 512:(hh + 1) * 512], in_=psF[bk][:, :], reads=[bF[bk]], writes=[b_bc])
    wbf = cx.sb([128, 8, D], BF16, "wbf_out"); b_wbf = Buf()
    for k in range(8):
        for hh in range(2):
            st, sb_ = G.wst.next()
            P.dma(st[:, :], drv["w_out"][k * 128:(k + 1) * 128, hh * 512:(hh + 1) * 512], writes=[sb_])
            P.op("pool", "tensor_copy", out=wbf[:, k, hh * 512:(hh + 1) * 512], in_=st[:, :], reads=[sb_], writes=[b_wbf])
    os_ = Rot([cx.sb([128, D], F32, f"os{i}") for i in range(2)])
    oT = Rot([cx.sb([128, 8, 128], BF16, f"oT{i}") for i in range(2)])
    zs = Rot([cx.sb([128, D], F32, f"zs{i}") for i in range(2)])
    ys = Rot([cx.sb([128, D], F32, f"ys{i}") for i in range(2)])
    st_r = Rot([cx.sb([128, 8], F32, f"st{i}") for i in range(2)])
    junk = cx.sb([128, D], F32, "junk"); b_junk = Buf()
    ident = G.c["ident_f"]
    def loads(ti):
        xt, xb = G.xs.next()
        ot, otb = os_.next()
        P.dma(xt[:, :], x_src[ti * 128:(ti + 1) * 128, :], reads=list(x_reads), writes=[xb])
        if gathered:
            for q_, (rk, c0) in enumerate(((0, 0), (1, 0), (0, 256), (1, 256))):
                P.dma(ot[:, q_ * 256:(q_ + 1) * 256], O.gathered_ap(ti, rk, c0), reads=[O.gbuf(ti)], writes=[otb])
        else:
            P.dma(ot[:, :], O[ti * 128:(ti + 1) * 128, :], reads=[b_O], writes=[otb])
        return xt, xb, ot, otb
    nxt_ld = loads(0)
    for ti in range(NT):
        xt, xb, ot, otb = nxt_ld
        if ti + 1 < NT:
            nxt_ld = loads(ti + 1)
        oTt, oTb = oT.next()
        for half in range(2):
            bk = (ti % 2) * 2 + half
            for j in range(4):
                k = half * 4 + j
                P.op("pe", "transpose", psF[bk][:, j * 128:(j + 1) * 128], ot[:, k * 128:(k + 1) * 128], ident[:, :], reads=[otb, G.bC], writes=[bF[bk]])
            if half:
                P.op("act", "activation", out=oTt[:, 4:8, :], in_=psF[bk][:, :].rearrange("p (a b) -> p a b", b=128), func=AF.Identity,
                     reads=[bF[bk]], writes=[oTb])
            else:
                P.op("dve", "tensor_copy", out=oTt[:, 0:4, :], in_=psF[bk][:, :].rearrange("p (a b) -> p a b", b=128), reads=[bF[bk]], writes=[oTb])
        zt, zb = zs.next()
        for hh in range(2):
            bk = 4 + hh
            for k in range(8):
                P.op("pe", "matmul", psF[bk][:, :], lhsT=oTt[:, k, :], rhs=wbf[:, k, hh * 512:(hh + 1) * 512], start=(k == 0), stop=(k == 7),
                     reads=[oTb, b_wbf], writes=[bF[bk]])
            P.op("dve", "tensor_tensor", out=zt[:, hh * 512:(hh + 1) * 512], in0=psF[bk][:, :], in1=bc[:, 0, hh * 512:(hh + 1) * 512], op=ALU.mult,
                 reads=[bF[bk], b_bc], writes=[zb])
        P.op("dve", "scalar_tensor_tensor", out=zt[:, :], in0=xt[:, :], scalar=ALPHA, in1=zt[:, :], op0=ALU.mult, op1=ALU.add,
             reads=[xb, zb], writes=[zb])
        stt, stb = st_r.next()
        P.op("dve", "reduce_sum", out=stt[:, 0:1], in_=zt[:, :], axis=AX.X, reads=[zb], writes=[stb])
        P.op("act", "activation", out=junk[:, :], in_=zt[:, :], func=AF.Square, accum_out=stt[:, 1:2], reads=[zb], writes=[b_junk, stb])
        P.op("dve", "tensor_scalar_mul", out=stt[:, 2:3], in0=stt[:, 0:1], scalar1=1.0 / D, reads=[stb], writes=[stb])
        P.op("dve", "tensor_tensor", out=stt[:, 3:4], in0=stt[:, 2:3], in1=stt[:, 2:3], op=ALU.mult, reads=[stb], writes=[stb])
        P.op("dve", "scalar_tensor_tensor", out=stt[:, 4:5], in0=stt[:, 1:2], scalar=1.0 / D, in1=stt[:, 3:4], op0=ALU.mult, op1=ALU.subtract,
             reads=[stb], writes=[stb])
        P.op("dve", "tensor_scalar_add", out=stt[:, 4:5], in0=stt[:, 4:5], scalar1=LN_EPS, reads=[stb], writes=[stb])
        P.op("act", "activation", out=stt[:, 5:6], in_=stt[:, 4:5], func=AF.Sqrt, reads=[stb], writes=[stb])
        P.op("dve", "reciprocal", out=stt[:, 6:7], in_=stt[:, 5:6], reads=[stb], writes=[stb])
        yt, yb = ys.next()
        P.op("dve", "tensor_scalar", out=yt[:, :], in0=zt[:, :], scalar1=stt[:, 2:3], scalar2=stt[:, 6:7], op0=ALU.subtract, op1=ALU.mult,
             reads=[zb, stb], writes=[yb])
        P.op("pool", "tensor_tensor", out=yt[:, :], in0=yt[:, :], in1=bc[:, 1, :], op=ALU.mult, reads=[yb, b_bc], writes=[yb])
        P.op("pool", "tensor_tensor", out=yt[:, :], in0=yt[:, :], in1=bc[:, 2, :], op=ALU.add, reads=[yb, b_bc], writes=[yb])
        tok = P.dma(dst[ti * 128:(ti + 1) * 128, :], yt[:, :], reads=[yb], writes=[b_dst] if b_dst is not None else [])
        if finals is not None:
            finals.append(tok)


PAIRS = [[0, 1], [2, 3], [4, 5], [6, 7]]
CH_ROWS = 1024
NCH = S // CH_ROWS
TPC = CH_ROWS // 128


class ChunkedO:
    def __init__(self, P, og_tensors, oa_tensors):
        self.P = P
        self.og_t, self.oa_t = og_tensors, oa_tensors
        self.og = [t.ap() for t in og_tensors]
        self.oa = [t.ap() for t in oa_tensors]
        self.b_og = [Buf() for _ in og_tensors]
        self.b_oa = [Buf() for _ in oa_tensors]
        self.sent = [False] * len(og_tensors)

    def ap(self, i, c0):
        k, r = i // TPC, (i % TPC) * 128
        return self.og[k][r:r + 128, c0:c0 + 256]

    def buf(self, i):
        return self.b_og[i // TPC]

    def _send(self, k):
        if not self.sent[k]:
            self.sent[k] = True
            self.P.coll("AllGather", PAIRS, self.og_t[k].ap().opt(), self.oa_t[k].ap().opt(), reads=[self.b_og[k]], writes=[self.b_oa[k]])

    def done(self, i):
        if self.armed and i % TPC == TPC - 1:
            self._send(i // TPC)

    armed = False

    def flush(self):
        for k in range(len(self.og)):
            self._send(k)

    def gathered_ap(self, ti, rank, c0):
        k, r = ti // TPC, (ti % TPC) * 128
        return self.oa[k][rank * CH_ROWS + r:rank * CH_ROWS + r + 128, c0:c0 + 256]

    def gbuf(self, ti):
        return self.b_oa[ti // TPC]


def build_fused2():
    nc = bass.Bass("TRN2", target_bir_lowering=False)
    dr = {}
    for nm, shp in (("x", [S, D]), ("cvec", [128, 8]), ("w_ada", [2, D, 3072]), ("b_ada", [2, 1, 3072]),
                    ("w_in_even", [D, EV_COLS]), ("w_in_odd", [D, OD_COLS]), ("w_out", [2, D, D]), ("lng", [2, 1, D]), ("lnb", [2, 1, D]),
                    ("sinks", [128, 4]), ("wukT", [128, 2, 128]), ("wuv", [128, 4, 64]),
                    ("cos2", [128, S]), ("sin2", [128, S]), ("cos32", [128, S]), ("sin32", [128, S]),
                    ("ident_f", [128, 128]), ("one", [65, 1]), ("ones_row", [1, 128]), ("zeros", [128, 512]), ("trif", [128, 128]), ("gbc", [128, 128])):
        _din(nc, dr, nm, shp)
    for nm, shp in (("ident_b", [128, 128]), ("tri", [128, 128]), ("tris", [128, 128]), ("band", [128, 256])):
        _din(nc, dr, nm, shp, BF16)
    y_out = nc.dram_tensor("y", [S, D], F32, kind="ExternalOutput").ap()
    X1 = nc.dram_tensor("x1_scratch", [S, D], F32, kind="Internal").ap()
    OGt = [[nc.dram_tensor(f"og{l}_{k}", [CH_ROWS, 512], F32) for k in range(NCH)] for l in range(2)]
    OAt = [[nc.dram_tensor(f"oa{l}_{k}", [2 * CH_ROWS, 512], F32) for k in range(NCH)] for l in range(2)]
    P = Prog(nc)
    Buf.default_r = {}
    finals = []
    with ExitStack() as es:
        extra = [("band", [128, 256], BF16), ("tris", [128, 128], BF16), ("zeros", [128, 512], F32), ("trif", [128, 128], F32),
                 ("gbc", [128, 128], F32), ("ones_row", [1, 128], F32)]
        G = setup_common(nc, P, es, dr, extra, wbf_cols=512)
        big = G.cx.sb([128, SLAB_ELEMS], BF16, "slab")
        SL = SlabCtx(big, SLAB_ELEMS)
        G.lcx = SL
        G.modc_t = G.cx.sb([128, 16], F32, "modc_p"); G.b_modc = Buf()
        G.grow = G.cx.sb([1, D], F32, "grow_p"); G.b_grow = Buf()
        b_X1 = Buf()
        for layer in range(2):
            x_src, x_reads = (dr["x"], []) if layer == 0 else (X1, [b_X1])
            OG = ChunkedO(P, OGt[layer], OAt[layer])
            b_OG = None
            _new_phase(P, SL, G)
            G.dr = {"cvec": dr["cvec"], "w_ada": dr["w_ada"][layer], "b_ada": dr["b_ada"][layer]}
            emit_adaln(P, G)
            P.dma(G.grow[0:1, :], G.modrow[64:65, :], reads=[G.b_modrow], writes=[G.b_grow])
            emit_hT(P, G, x_src, x_reads)
            if layer == 0:
                drv = {"w_in": dr["w_in_even"], "sinks": dr["sinks"], "cos2": dr["cos2"], "sin2": dr["sin2"]}
                OG.armed = True
                phase_even(P, G, SL, drv, OG, b_OG, 0, colA=0, colB=256)
            else:
                drv = {"w_in": dr["w_in_odd"], "cos32": dr["cos32"], "sin32": dr["sin32"], "wukT": dr["wukT"], "wuv": dr["wuv"]}
                phase_sb(P, G, SL, drv, OG, b_OG, 0, col=256)
                OG.armed = True
                phase_dsa(P, G, SL, drv, OG, b_OG, 0, None, None, "solo", col=0)
            OG.flush()
            OA, b_OA = OG, None
            drv = {"w_out": dr["w_out"][layer], "lng": dr["lng"][layer], "lnb": dr["lnb"][layer]}
            if layer == 0:
                phase_out(P, G, SL, drv, x_src, x_reads, OA, b_OA, X1, b_X1, gathered=True)
            else:
                phase_out(P, G, SL, drv, x_src, x_reads, OA, b_OA, y_out, None, finals, gathered=True)
        P.emit(final_waits=finals)
    Buf.default_r = {}
    return nc


def _fused2_inputs(x, c, w_ada, b_ada, w_in_even, sinks, w_in_odd, kv_g, w_uk, w_uv, w_out, ln_g, ln_b):
    ev = _even_inputs(x, c, w_ada[0], b_ada[0], w_in_even, sinks)
    od = _odd_inputs(x, c, w_ada[1], b_ada[1], w_in_odd, kv_g, w_uk, w_uv)
    shared = {
        "w_ada": np.ascontiguousarray(w_ada), "b_ada": np.ascontiguousarray(b_ada.reshape(2, 1, 3072)),
        "w_out": np.ascontiguousarray(w_out), "lng": np.ascontiguousarray(ln_g.reshape(2, 1, D)), "lnb": np.ascontiguousarray(ln_b.reshape(2, 1, D)),
        "ones_row": np.ones((1, 128), np.float32),
    }
    for k in ("cos2", "sin2", "ident_f", "one", "ident_b", "tri", "band"):
        shared[k] = ev[0][k]
    for k in ("cos32", "sin32", "tris", "zeros", "trif", "gbc"):
        shared[k] = od[0][k]
    maps = []
    for core in range(8):
        m = dict(shared)
        m["x"] = ev[core]["x"]
        m["cvec"] = ev[core]["cvec"]
        m["w_in_even"] = ev[core]["w_in"]
        m["sinks"] = ev[core]["sinks"]
        m["w_in_odd"] = od[core]["w_in"]
        m["wukT"] = od[core]["wukT"]
        m["wuv"] = od[core]["wuv"]
        maps.append(m)
    return maps


def kernel(x, c, w_ada, b_ada, w_in_even, sink_logits, w_in_odd, kv_norm_g, w_uk, w_uv, w_out, ln_g, ln_b):
    f = lambda a: np.asarray(a, np.float32)
    maps = _fused2_inputs(f(x), f(c), f(w_ada), f(b_ada), f(w_in_even[0]), f(sink_logits[0]), f(w_in_odd[0]), f(kv_norm_g[0]),
                          f(w_uk[0]), f(w_uv[0]), f(w_out), f(ln_g), f(ln_b))
    res = _run(_get_nc("fused2", build_fused2), maps)
    return np.stack([res[2 * b]["y"] for b in range(4)]).astype(np.float32)
```

```python
from contextlib import ExitStack
import numpy as np
import concourse.bass as bass
import concourse.mybir as mybir
from concourse.bass_utils import run_bass_kernel_spmd

F32 = mybir.dt.float32
BF16 = mybir.dt.bfloat16
AF = mybir.ActivationFunctionType
ALU = mybir.AluOpType
AX = mybir.AxisListType

S = 4096
D = 1024
NT = S // 128
NEG = -30000.0
ALPHA = 4.0 ** 0.25
LN_EPS = 1e-5

ENGS = ("pe", "act", "dve", "pool", "sp")
EP = 24000
NDMA = 20


class Buf:
    __slots__ = ("name", "w", "r", "excl")
    default_r = {}

    def __init__(self, name="", excl=False):
        self.name = name
        self.w = {}
        self.r = dict(Buf.default_r)
        self.excl = excl


def _tok_key(t):
    return t[0] if t[0] not in ("dma", "cc") else (t[0], t[1])


class Prog:
    def __init__(self, nc):
        self.nc = nc
        self.q = {e: [] for e in ENGS}
        self.cnt = {e: 0 for e in ENGS}
        self.seen = {e: {} for e in ENGS}
        self.ndma = 0

    def _deps(self, eng, reads, writes, extra, skip_self):
        deps = {}

        def add(t):
            k = _tok_key(t)
            if skip_self and k == eng:
                return
            if k not in deps or t[1:] >= deps[k][1:]:
                deps[k] = t
        for b in reads:
            for t in b.w.values():
                add(t)
            if b.excl:
                for k_, t in b.r.items():
                    if k_ != eng:
                        add(t)
        for b in writes:
            for t in b.w.values():
                add(t)
            for t in b.r.values():
                add(t)
        for t in extra:
            if t is not None:
                add(t)
        waits = []
        seen = self.seen[eng]
        for k, t in deps.items():
            if k in seen and seen[k][1:] >= t[1:]:
                continue
            seen[k] = t
            waits.append(t)
        return waits

    def _mark(self, tok, reads, writes):
        k = _tok_key(tok)
        for b in reads:
            b.r[k] = tok
        for b in writes:
            b.w = {k: tok}
            b.r = {}

    def op(self, eng, name, *args, reads=(), writes=(), extra=(), skip_self=None, **kw):
        fn = (name, args, kw)
        if skip_self is None:
            skip_self = (eng == "pe")
        waits = self._deps(eng, reads, writes, extra, skip_self)
        c = self.cnt[eng]
        self.cnt[eng] = c + 1
        tok = (eng, c // EP, c % EP + 1)
        self.q[eng].append((waits, fn, tok))
        self._mark(tok, reads, writes)
        return tok

    def dma(self, out, in_, reads=(), writes=(), extra=(), queue="sp"):
        n = self.ndma
        self.ndma += 1
        slot, rnd = n % NDMA, n // NDMA
        tok = ("dma", slot, rnd + 1)
        extra = list(extra)
        if rnd > 0:
            extra.append(("dma", slot, rnd))
        waits = self._deps(queue, reads, writes, extra, False)
        self.q[queue].append((waits, ("dma", out, in_), tok))
        self._mark(tok, reads, writes)
        return tok

    def coll(self, kind, groups, in_ap, out_ap, reads=(), writes=()):
        idx = getattr(self, "ncc", 0)
        self.ncc = idx + 1
        tok = ("cc", idx, 1)
        waits = self._deps("pool", reads, writes, (), False)
        self.q["pool"].append((waits, ("cc", kind, groups, in_ap, out_ap), tok))
        self._mark(tok, reads, writes)
        return tok

    def fence(self):
        toks = {}
        for e in ("pe", "act", "dve", "pool"):
            c = self.cnt[e]
            if c > 0:
                toks[e] = (e, (c - 1) // EP, (c - 1) % EP + 1)
        for slot in range(min(NDMA, self.ndma)):
            last = ((self.ndma - 1 - slot) // NDMA) * NDMA + slot
            toks[("dma", slot)] = ("dma", slot, last // NDMA + 1)
        for s in range(getattr(self, "ncc", 0)):
            toks[("cc", s)] = ("cc", s, 1)
        return toks

    def emit(self, final_waits=()):
        nc = self.nc
        with ExitStack() as es:
            sems = {}
            for e in ENGS:
                nep = (self.cnt[e] + EP - 1) // EP
                for ep in range(max(nep, 1)):
                    sems[(e, ep)] = es.enter_context(nc.semaphore(f"s_{e}_{ep}"))
            for s in range(NDMA):
                sems[("dma", s)] = es.enter_context(nc.semaphore(f"s_dma_{s}"))
            for s in range(getattr(self, "ncc", 0)):
                sems[("cc", s)] = es.enter_context(nc.semaphore(f"s_cc_{s}"))
            block = es.enter_context(nc.Block())

            def wait(engobj, t):
                if t[0] == "cc":
                    engobj.wait_ge(sems[("cc", t[1])], 1)
                elif t[0] == "dma":
                    engobj.wait_ge(sems[("dma", t[1])], 16 * t[2])
                else:
                    engobj.wait_ge(sems[(t[0], t[1])], t[2])

            def mk(e):
                def body(engobj):
                    for waits, fn, tok in self.q[e]:
                        for t in waits:
                            wait(engobj, t)
                        if fn[0] == "cc":
                            _, kind, groups, in_ap, out_ap = fn
                            engobj.collective_compute(kind, ALU.bypass, replica_groups=groups, ins=[in_ap], outs=[out_ap]).then_inc(
                                sems[("cc", tok[1])])
                        elif fn[0] == "dma":
                            _, out, in_ = fn
                            engobj.dma_start(out=out, in_=in_).then_inc(sems[("dma", tok[1])], 16)
                        else:
                            getattr(engobj, fn[0])(*fn[1], **fn[2]).then_inc(sems[(tok[0], tok[1])], 1)
                    if e == "sp":
                        for t in final_waits:
                            wait(engobj, t)
                return body
            block.tensor(mk("pe"))
            block.scalar(mk("act"))
            block.vector(mk("dve"))
            block.gpsimd(mk("pool"))
            block.sync(mk("sp"))


class Ctx:
    def __init__(self, nc, es):
        self.nc = nc
        self.es = es
        self.n = 0

    def sb(self, shape, dt, name=None):
        self.n += 1
        t = self.es.enter_context(self.nc.sbuf_tensor("sb_" + (name or f"t{self.n}"), list(shape), dt))
        return t

    def ps(self, shape, dt, name=None):
        self.n += 1
        t = self.es.enter_context(self.nc.psum_tensor("ps_" + (name or f"p{self.n}"), list(shape), dt))
        return t


class SlabCtx:
    def __init__(self, big, nelem):
        self.big = big
        self.nelem = nelem
        self.off = 0

    def reset(self):
        self.off = 0

    def sb(self, shape, dt, name=None):
        n = 1
        for s_ in shape[1:]:
            n *= s_
        nbf = n * (2 if dt == F32 else 1)
        off = (self.off + 31) // 32 * 32
        assert off + nbf <= self.nelem, f"slab overflow for {name}: {off + nbf} > {self.nelem}"
        v = self.big[0:shape[0], off:off + nbf]
        if dt == F32:
            v = v.bitcast(F32)
        if len(shape) == 3:
            v = v.rearrange("p (a b) -> p a b", b=shape[2])
        self.off = off + nbf
        return v


class Rot:
    def __init__(self, tensors):
        self.t = tensors
        self.b = [Buf() for _ in tensors]
        self.i = -1

    def next(self):
        self.i = (self.i + 1) % len(self.t)
        return self.t[self.i], self.b[self.i]


class NS:
    pass


def setup_common(nc, P, es, dr, extra_consts=(), wbf_cols=1024):
    G = NS()
    cx = Ctx(nc, es)
    G.cx = cx
    G.dr = dr
    G.psF = [cx.ps([128, 512], F32, f"psF{i}") for i in range(6)]
    G.bF = [Buf(excl=True) for _ in range(6)]
    G.psB = [cx.ps([128, 1024], BF16, f"psB{i}") for i in range(2)]
    G.bB = [Buf(excl=True) for _ in range(2)]
    cl = [("ident_f", [128, 128], F32), ("ident_b", [128, 128], BF16), ("one", [65, 1], F32), ("tri", [128, 128], BF16)] + list(extra_consts)
    G.c = {}
    G.bC = Buf()
    for nm, shape, dt in cl:
        t = cx.sb(shape, dt, "c_" + nm)
        sl = tuple(slice(None) for _ in shape)
        P.dma(t[sl], dr[nm][sl], writes=[G.bC])
        G.c[nm] = t
    G.hT = cx.sb([128, 8, S], BF16, "hT"); G.b_hT = Buf()
    G.wst = Rot([cx.sb([128, 512], F32, f"wst{i}") for i in range(2)])
    G.xs = Rot([cx.sb([128, 1024], F32, f"xs{i}") for i in range(2)])
    G.rope = Rot([cx.sb([128, 2, 512], F32, f"rope{i}") for i in range(2)])
    G.wbf = cx.sb([128, 8, wbf_cols], BF16, "wbf"); G.b_wbf = Buf()
    G.osb = Rot([cx.sb([128, 512], F32, f"osb{i}") for i in range(2)])
    G.p_sb = Rot([cx.sb([128, 512], BF16, f"p_sb{i}") for i in range(2)])
    G.pt_sb = Rot([cx.sb([128, 512], BF16, f"pt_sb{i}") for i in range(2)])
    G.rs = Rot([cx.sb([128, 16], F32, f"rs{i}") for i in range(3)])
    G.den = Rot([cx.sb([128, 4], F32, f"den{i}") for i in range(3)])
    G.pipe = Pipe()
    G.ptb_i = [0]
    return G


def emit_adaln(P, G, want_cols=True):
    cx, dr = getattr(G, "lcx", G.cx), G.dr
    cvec = cx.sb([128, 8], F32, "cvec"); cond = cx.sb([128, 8], F32, "cond")
    b_cv, b_cond = Buf(), Buf()
    P.dma(cvec[:, :], dr["cvec"][:, :], writes=[b_cv])
    P.op("act", "activation", out=cond[:, :], in_=cvec[:, :], func=AF.Silu, reads=[b_cv], writes=[b_cond])
    modrow, brow = G.xs.t[0], G.xs.t[1]
    b_modrow, b_brow = G.xs.b[0], G.xs.b[1]
    for r in range(3):
        P.dma(brow[32 * r:32 * r + 1, :], dr["b_ada"][0:1, r * 1024:(r + 1) * 1024], writes=[b_brow])
    wst = G.wst
    if getattr(G, "lcx", None) is not None:
        wst = Rot([cx.sb([128, 512], F32, f"adast{i}") for i in range(8)])
    for j in range(6):
        for k in range(8):
            wt, wb = wst.next()
            P.dma(wt[:, :], dr["w_ada"][k * 128:(k + 1) * 128, j * 512:(j + 1) * 512], writes=[wb])
            P.op("pe", "matmul", G.psF[j][0:1, :], lhsT=cond[:, k:k + 1], rhs=wt[:, :],
                 start=(k == 0), stop=(k == 7), reads=[wb, b_cond], writes=[G.bF[j]])
    for j in range(6):
        r = 32 * (j // 2)
        cs = slice((j % 2) * 512, (j % 2) * 512 + 512)
        if j < 2:
            P.op("dve", "tensor_tensor", out=modrow[r:r + 1, cs], in0=G.psF[j][0:1, :], in1=brow[r:r + 1, cs], op=ALU.add,
                 reads=[G.bF[j], b_brow], writes=[b_modrow])
        else:
            P.op("dve", "scalar_tensor_tensor", out=modrow[r:r + 1, cs], in0=G.psF[j][0:1, :], scalar=1.0, in1=brow[r:r + 1, cs],
                 op0=ALU.add, op1=ALU.add, reads=[G.bF[j], b_brow], writes=[b_modrow])
    G.modrow, G.b_modrow = modrow, b_modrow
    if not want_cols:
        return
    modc = getattr(G, "modc_t", None)
    if modc is None:
        modc = cx.sb([128, 16], F32, "modc")
    b_modc = getattr(G, "b_modc", None) or Buf()
    pc = G.psF[0]
    for j in range(16):
        r = 32 * (j // 8)
        P.op("pe", "matmul", pc[:, j:j + 1], lhsT=modrow[r:r + 1, (j % 8) * 128:(j % 8 + 1) * 128], rhs=G.c["one"][r:r + 1, 0:1],
             start=True, stop=True, reads=[b_modrow, G.bC], writes=[G.bF[0]])
    P.op("dve", "tensor_copy", out=modc[:, :], in_=pc[:, 0:16], reads=[G.bF[0]], writes=[b_modc])
    G.modc, G.b_modc = modc, b_modc


def emit_hT(P, G, x_dram, x_reads=()):
    hT, modc = G.hT, G.modc
    bank = 0
    for ti in range(NT):
        xt, xb = G.xs.next()
        P.dma(xt[:, :], x_dram[ti * 128:(ti + 1) * 128, :], reads=list(x_reads), writes=[xb])
        for half in range(2):
            pt, pb = G.psF[bank], G.bF[bank]
            bank = (bank + 1) % 4
            for j in range(4):
                k = half * 4 + j
                P.op("pe", "transpose", pt[:, j * 128:(j + 1) * 128], xt[:, k * 128:(k + 1) * 128], G.c["ident_f"][:, :],
                     reads=[xb, G.bC], writes=[pb])
            for j in range(4):
                k = half * 4 + j
                if j % 2 == 0:
                    P.op("dve", "tensor_scalar",
                        out=hT[:, k, ti * 128:(ti + 1) * 128], in0=pt[:, j * 128:(j + 1) * 128],
                        scalar1=modc[:, 8 + k:9 + k], scalar2=modc[:, k:k + 1], op0=ALU.mult, op1=ALU.add,
                        reads=[pb, G.b_modc], writes=[G.b_hT])
                else:
                    P.op("act", "activation",
                        out=hT[:, k, ti * 128:(ti + 1) * 128], in_=pt[:, j * 128:(j + 1) * 128],
                        func=AF.Identity, scale=modc[:, 8 + k:9 + k], bias=modc[:, k:k + 1],
                        reads=[pb, G.b_modc], writes=[G.b_hT])


def load_w(P, G, segs, wbf=None, b_wbf=None):
    wbf = G.wbf if wbf is None else wbf
    b_wbf = G.b_wbf if b_wbf is None else b_wbf
    tot = sum(n for _, n in segs)
    assert tot <= wbf.shape[2] and tot <= 512
    wst = getattr(G, "wst_deep", None) or G.wst
    for k in range(8):
        st, sb_ = wst.next()
        o = 0
        for c0, n in segs:
            P.dma(st[:, o:o + n], G.dr["w_in"][k * 128:(k + 1) * 128, c0:c0 + n], writes=[sb_])
            o += n
        P.op("pool", "tensor_copy", out=wbf[:, k, 0:tot], in_=st[:, 0:tot], reads=[sb_], writes=[b_wbf])


def fm_proj(P, G, wc0, M, tc, bank):
    ps, pb = G.psF[bank], G.bF[bank]
    for k in range(8):
        P.op("pe", "matmul", ps[0:M, :], lhsT=G.wbf[:, k, wc0:wc0 + M], rhs=G.hT[:, k, tc * 512:(tc + 1) * 512],
                                           start=(k == 0), stop=(k == 7), reads=[G.b_hT, G.b_wbf], writes=[pb])
    return ps, pb


def fm_proj_rope(P, G, wc_plain, wc_perm, M, dst_fn, b_dst, cosn, sinn, bank0):
    for tc in range(S // 512):
        rp, rb = G.rope.next()
        P.dma(rp[0:M, 0, :], G.dr[cosn][0:M, tc * 512:(tc + 1) * 512], writes=[rb])
        P.dma(rp[0:M, 1, :], G.dr[sinn][0:M, tc * 512:(tc + 1) * 512], writes=[rb])
        ps1, pb1 = fm_proj(P, G, wc_plain, M, tc, bank0 + (tc % 2) * 2)
        ps2, pb2 = fm_proj(P, G, wc_perm, M, tc, bank0 + (tc % 2) * 2 + 1)
        t1, tb1 = G.xs.next()
        P.op("dve", "tensor_tensor", out=t1[0:M, 0:512], in0=ps1[0:M, :], in1=rp[0:M, 0, :], op=ALU.mult,
             reads=[pb1, rb], writes=[tb1])
        P.op("dve", "tensor_tensor", out=t1[0:M, 512:1024], in0=ps2[0:M, :], in1=rp[0:M, 1, :], op=ALU.mult,
             reads=[pb2, rb], writes=[tb1])
        P.op("pool", "tensor_tensor", out=dst_fn(tc), in0=t1[0:M, 0:512], in1=t1[0:M, 512:1024], op=ALU.add,
             reads=[tb1], writes=[b_dst])


def tm_proj(P, G, wbf, b_wbf, wc0, N, ti, bank):
    ps, pb = G.psF[bank], G.bF[bank]
    for k in range(8):
        P.op("pe", "matmul", ps[:, 0:N], lhsT=G.hT[:, k, ti * 128:(ti + 1) * 128], rhs=wbf[:, k, wc0:wc0 + N],
                                           start=(k == 0), stop=(k == 7), reads=[G.b_hT, b_wbf], writes=[pb])
    return ps, pb


def deepen_wst(G, cx, nextra):
    r = Rot(list(G.wst.t) + [cx.sb([128, 512], F32, f"wstx{i}") for i in range(nextra)])
    r.b[0], r.b[1] = G.wst.b[0], G.wst.b[1]
    G.wst_deep = r


class Pipe:
    def __init__(self):
        self.items = []
        self.done = [0, 0, 0]

    def push(self, stages):
        self.items.append(stages)
        n = len(self.items) - 1
        for k in range(3):
            j = n - k
            if j >= 0 and self.done[k] == j:
                self.items[j][k]()
                self.done[k] = j + 1

    def flush(self):
        n = len(self.items)
        for j in range(n):
            for k in range(3):
                if self.done[k] <= j:
                    assert self.done[k] == j
                    self.items[j][k]()
                    self.done[k] = j + 1
        self.items = []
        self.done = [0, 0, 0]


def attn_qtile(P, G, s_parts, s_reads, k0, nkt, addmask_fn, bias_fn, bias_reads, v_fn, v_reads, dv, sbank, obank, scale, blk=4, epi=None):
    pipe = G.pipe
    rs, rsb = G.rs.next()
    P.op("pool", "memset", rs[:, :], 0.0, writes=[rsb])
    op_, opb = G.psF[obank], G.bF[obank]
    den, db = G.den.next()
    col = 0
    chunks = list(range(k0, nkt, 4))
    for ci, c0 in enumerate(chunks):
        c1 = min(c0 + 4, nkt)
        w = (c1 - c0) * 128
        sp_, spb = G.psF[sbank[0]], G.bF[sbank[0]]
        sbank[0], sbank[1] = sbank[1], sbank[0]
        pch, pcb = G.p_sb.next()
        bi = G.ptb_i[0] % 2
        G.ptb_i[0] += 1
        ptp, ptb = G.psB[bi], G.bB[bi]
        pts, ptsb = G.pt_sb.next()
        col0 = col
        nblk = (c1 - c0 + blk - 1) // blk
        col += nblk
        assert col <= 16
        last = (ci == len(chunks) - 1)
        first = (ci == 0)

        def stageA(c0=c0, c1=c1, w=w, sp_=sp_, spb=spb, pch=pch, pcb=pcb, col0=col0):
            masks = addmask_fn(c0, c1)
            for pi, part in enumerate(s_parts):
                lhsT, rhs_fn = part[0], part[1]
                pkw = part[2] if len(part) > 2 else {}
                P.op("pe", "matmul", sp_[:, 0:w], lhsT=lhsT, rhs=rhs_fn(c0 * 128, w), start=(pi == 0),
                     stop=(pi == len(s_parts) - 1 and not masks), reads=s_reads, writes=[spb], **pkw)
            for mi, (lo, mw, rhs_ap, mreads) in enumerate(masks):
                P.op("pe", "matmul", sp_[:, lo:lo + mw], lhsT=G.c["ident_b"][:, :], rhs=rhs_ap, start=False, stop=(mi == len(masks) - 1),
                     reads=[G.bC] + list(mreads), writes=[spb])
            for _d in range(getattr(G, "dummy_mm", 0)):
                P.op("pe", "matmul", G.psF[5][:, 0:256], lhsT=G.c["ident_b"][:, :], rhs=G.c["band"][:, :], start=True, stop=True,
                     reads=[G.bC], writes=[G.bF[5]])
            cc = col0
            for kb0 in range(c0, c1, blk):
                kb1 = min(kb0 + blk, c1)
                lo, hi = (kb0 - c0) * 128, (kb1 - c0) * 128
                b_ap = bias_fn(kb0) if bias_fn is not None else None
                if b_ap is not None:
                    P.op("act", "activation", out=pch[:, lo:hi], in_=sp_[:, lo:hi], func=AF.Exp, scale=scale, bias=b_ap,
                         accum_out=rs[:, cc:cc + 1], reads=[spb] + list(bias_reads), writes=[pcb, rsb], skip_self=True)
                else:
                    P.op("act", "activation", out=pch[:, lo:hi], in_=sp_[:, lo:hi], func=AF.Exp, scale=scale,
                         accum_out=rs[:, cc:cc + 1], reads=[spb], writes=[pcb, rsb], skip_self=True)
                cc += 1

        def stageB(c0=c0, c1=c1, w=w, pch=pch, pcb=pcb, ptp=ptp, ptb=ptb, pts=pts, ptsb=ptsb):
            for kk in range(c1 - c0):
                P.op("pe", "transpose", ptp[:, kk * 128:(kk + 1) * 128], pch[:, kk * 128:(kk + 1) * 128], G.c["ident_b"][:, :],
                     reads=[pcb, G.bC], writes=[ptb])
            if getattr(G, "evac_eng", "dve") == "act":
                P.op("act", "activation", out=pts[:, 0:w], in_=ptp[:, 0:w], func=AF.Identity, reads=[ptb], writes=[ptsb])
            else:
                P.op("dve", "tensor_copy", out=pts[:, 0:w], in_=ptp[:, 0:w], reads=[ptb], writes=[ptsb])

        def stageC(c0=c0, c1=c1, pts=pts, ptsb=ptsb, first=first, last=last):
            for kk in range(c1 - c0):
                kt = c0 + kk
                P.op("pe", "matmul", op_[:, 0:dv], lhsT=pts[:, kk * 128:(kk + 1) * 128], rhs=v_fn(kt), start=(first and kk == 0),
                     stop=(kt == nkt - 1), reads=[ptsb] + list(v_reads), writes=[opb])
            if last:
                P.op("dve", "reduce_sum", out=den[:, 0:1], in_=rs[:, :], axis=AX.X, reads=[rsb], writes=[db])
                if epi is not None:
                    epi(op_, opb, den, db)
        pipe.push([stageA, stageB, stageC])
        if getattr(G, "after_push", None) is not None:
            G.after_push()
    return op_, opb, den, db


EV_COLS = 2624


def _din(nc, dr, name, shape, dt=F32):
    dr[name] = nc.dram_tensor(name, list(shape), dt, kind="ExternalInput").ap()


TOK = 2048


OD_COLS = 2952
DSA_SCALE = 96.0 ** -0.5


def _odd_common(nc, name_extra, wbf_cols):
    dr = {}
    for nm, shp in (("x", [S, D]), ("cvec", [128, 8]), ("w_ada", [D, 3072]), ("b_ada", [1, 3072]), ("w_in", [D, OD_COLS]),
                    ("cos32", [128, S]), ("sin32", [128, S]), ("ident_f", [128, 128]), ("one", [65, 1])):
        _din(nc, dr, nm, shp)
    for nm, shp in (("ident_b", [128, 128]), ("tri", [128, 128]), ("tris", [128, 128])):
        _din(nc, dr, nm, shp, BF16)
    for nm, shp, dt in name_extra:
        _din(nc, dr, nm, shp, dt)
    return dr


def fm_plain_to(P, G, wc0, dst_fn, b_dst):
    for tc in range(S // 512):
        bank = tc % 4
        ps, pb = fm_proj(P, G, wc0, 128, tc, bank)
        if bank % 2 == 0:
            P.op("dve", "tensor_copy", out=dst_fn(tc), in_=ps[:, :], reads=[pb], writes=[b_dst])
        else:
            P.op("act", "activation", out=dst_fn(tc), in_=ps[:, :], func=AF.Identity, reads=[pb], writes=[b_dst])


import ml_dtypes
_BF = ml_dtypes.bfloat16


def _rope_np(seq, dim):
    inv = (1.0 / (np.float32(10000.0) ** (np.arange(0, dim, 2, dtype=np.float32) / np.float32(dim)))).astype(np.float32)
    ang = np.arange(seq, dtype=np.float32)[:, None] * inv[None, :]
    return np.cos(ang).astype(np.float32), np.sin(ang).astype(np.float32)


def _rope_tables_T(seq, dim, reps):
    cos, sin = _rope_np(seq, dim)
    c2 = np.concatenate([cos, cos], axis=1).T
    s2 = np.concatenate([-sin, sin], axis=1).T
    return np.ascontiguousarray(np.tile(c2, (reps, 1))), np.ascontiguousarray(np.tile(s2, (reps, 1)))


def _perm_heads(w, nh, dh):
    w = w.reshape(w.shape[0], nh, dh)
    return np.concatenate([w[:, :, dh // 2:], w[:, :, :dh // 2]], axis=2).reshape(w.shape[0], nh * dh)


def _consts():
    q = np.arange(128)[:, None]
    k = np.arange(128)[None, :]
    negm = NEG * 8.0
    tri = np.where(k <= q, 0.0, negm).astype(np.float32)
    prev = np.where(k > q, 0.0, negm).astype(np.float32)
    return {
        "ident_f": np.eye(128, dtype=np.float32),
        "ident_b": np.eye(128, dtype=np.float32).astype(_BF),
        "one": np.ones((65, 1), np.float32),
        "tri": tri.astype(_BF),
        "band": np.concatenate([prev, tri], axis=1).astype(_BF),
    }


_NC_CACHE = {}


def _get_nc(name, fn):
    if name not in _NC_CACHE:
        _NC_CACHE[name] = fn()
    return _NC_CACHE[name]


def _even_inputs(x, c, w_ada_l, b_ada_l, w_in, sinks):
    cst = _consts()
    cos2, sin2 = _rope_tables_T(S, 64, 2)
    maps = []
    for core in range(8):
        b, g = core // 2, core % 2
        A = lambda c0: w_in[:, c0 + 256 * g: c0 + 256 * g + 256]
        aq, ak, av, bq = A(0), A(512), A(1024), A(1536)
        bk = w_in[:, 2048 + 64 * g: 2048 + 64 * g + 64]
        bv = w_in[:, 2176 + 64 * g: 2176 + 64 * g + 64]
        ga = w_in[:, 2304 + 256 * g: 2304 + 256 * g + 256]
        gb = w_in[:, 2816 + 256 * g: 2816 + 256 * g + 256]
        bkp = _perm_heads(bk, 1, 64)
        wc = np.concatenate([aq, _perm_heads(aq, 4, 64), ak, _perm_heads(ak, 4, 64), bq, _perm_heads(bq, 4, 64),
                             bk, bk, bkp, bkp, av, bv, ga, gb], axis=1)
        assert wc.shape[1] == EV_COLS
        m = dict(cst)
        m.update({
            "x": np.ascontiguousarray(x[b]), "cvec": np.ascontiguousarray(c[b].reshape(8, 128).T),
            "w_ada": np.ascontiguousarray(w_ada_l), "b_ada": np.ascontiguousarray(b_ada_l.reshape(1, 3072)),
            "w_in": np.ascontiguousarray(wc), "cos2": cos2, "sin2": sin2,
            "sinks": np.ascontiguousarray(np.broadcast_to(sinks[4 * g:4 * g + 4][None, :], (128, 4))).astype(np.float32),
        })
        maps.append(m)
    return maps


def _run(nc, maps):
    res = run_bass_kernel_spmd(nc, maps, core_ids=list(range(8)))
    return res.results


def _odd_inputs(x, c, w_ada_l, b_ada_l, w_in, kv_g, w_uk, w_uv):
    cst = _consts()
    q = np.arange(128)[:, None]; k = np.arange(128)[None, :]
    cst["tris"] = np.where(k < q, 0.0, NEG * 8.0).astype(np.float32).astype(_BF)
    cst["zeros"] = np.zeros((128, 512), np.float32)
    cst["trif"] = np.where(k <= q, 0.0, -1e30).astype(np.float32)
    del cst["band"]
    cos32, sin32 = _rope_tables_T(S, 32, 4)
    maps = []
    for core in range(8):
        b, g = core // 2, core % 2
        H4 = lambda c0, dh: w_in[:, c0 + 4 * dh * g: c0 + 4 * dh * g + 4 * dh]
        cqn = H4(0, 64)
        cqr = H4(512, 32)
        ckv = w_in[:, 768:896]
        ckr = w_in[:, 896:928]
        iq = w_in[:, 928:1184]
        ik = w_in[:, 1184:1216]
        iw = w_in[:, 1216:1224]
        dq, dk, dv = H4(1224, 64), H4(1736, 64), H4(2248, 64)
        gc = w_in[:, 2760 + 256 * g: 2760 + 256 * g + 256]
        gd = w_in[:, 3272 + 256 * g: 3272 + 256 * g + 256]
        ckrp = _perm_heads(ckr, 1, 32)
        ikp = _perm_heads(ik, 1, 32)
        wc = np.concatenate([dq, dk, dv, gd, cqn, cqr, _perm_heads(cqr, 4, 32), ckr, ckr, ckr, ckr, ckrp, ckrp, ckrp, ckrp,
                             iq, _perm_heads(iq, 8, 32), ik, ik, ik, ik, ikp, ikp, ikp, ikp, ckv, iw, gc], axis=1)
        assert wc.shape[1] == OD_COLS, wc.shape
        wukT = np.zeros((128, 2, 128), np.float32)
        wuv = np.zeros((128, 4, 64), np.float32)
        for h in range(4):
            wukT[(h % 2) * 64:(h % 2) * 64 + 64, h // 2, :] = w_uk[4 * g + h].T
            wuv[:, h, :] = w_uv[4 * g + h]
        m = dict(cst)
        m.update({
            "x": np.ascontiguousarray(x[b]), "cvec": np.ascontiguousarray(c[b].reshape(8, 128).T),
            "w_ada": np.ascontiguousarray(w_ada_l), "b_ada": np.ascontiguousarray(b_ada_l.reshape(1, 3072)),
            "w_in": np.ascontiguousarray(wc), "cos32": cos32, "sin32": sin32,
            "gbc": np.ascontiguousarray(np.broadcast_to(kv_g[None, :], (128, 128))).astype(np.float32),
            "wukT": wukT, "wuv": wuv,
        })
        maps.append(m)
    return maps


SLAB_ELEMS = 12 * S


def _new_phase(P, SL, G=None):
    SL.reset()
    if G is not None:
        G.wst_deep = None
    Buf.default_r = P.fence()


def phase_even(P, G, SL, drv, O, b_O, g, colA=None, colB=None):
    colA = 256 * g if colA is None else colA
    colB = 512 + 256 * g if colB is None else colB
    _new_phase(P, SL, G)
    dr_save = G.dr
    G.dr = drv
    cx = SL
    sinks_t = cx.sb([128, 4], F32, "sinks"); b_sk = Buf()
    P.dma(sinks_t[:, :], drv["sinks"][:, :], writes=[b_sk])
    esink = cx.sb([128, 4], F32, "esink"); b_esink = Buf()
    P.op("act", "activation", out=esink[:, :], in_=sinks_t[:, :], func=AF.Exp, reads=[b_sk], writes=[b_esink])
    Q = cx.sb([128, 2, S], BF16, "Q"); b_Q = Buf()
    K = cx.sb([128, 2, S], BF16, "K"); b_K = Buf()
    V = cx.sb([128, NT, 256], BF16, "V"); b_V = Buf()
    SG = cx.sb([128, NT, 256], BF16, "SG"); b_SG = Buf()
    kmf = cx.sb([128, 2, 16], F32, "kmf"); b_kmf = Buf()
    kmb = cx.sb([128, 2, 16], BF16, "kmb"); b_kmb = Buf()
    gsb_r = Rot([cx.sb([128, 16], F32, f"gsb{i}") for i in range(2)])
    top_r = Rot([cx.sb([128, 8], F32, f"top{i}") for i in range(2)])
    bias_r = Rot([cx.sb([128, 16], F32, f"bias{i}") for i in range(2)])
    deepen_wst(G, cx, 6)
    load_w(P, G, [(0, 512)])
    for g2 in range(2):
        fm_proj_rope(P, G, g2 * 128, 256 + g2 * 128, 128, lambda tc, g2=g2: Q[:, g2, tc * 512:(tc + 1) * 512], b_Q, "cos2", "sin2", 0)
    load_w(P, G, [(512, 512)])
    for g2 in range(2):
        fm_proj_rope(P, G, g2 * 128, 256 + g2 * 128, 128, lambda tc, g2=g2: K[:, g2, tc * 512:(tc + 1) * 512], b_K, "cos2", "sin2", 0)
    load_w(P, G, [(1792, 256), (2112, 256)])
    for ti in range(NT):
        ps, pb = tm_proj(P, G, G.wbf, G.b_wbf, 0, 512, ti, ti % 4)
        P.op("dve", "tensor_copy", out=V[:, ti, :], in_=ps[:, 0:256], reads=[pb], writes=[b_V])
        P.op("act", "activation", out=SG[:, ti, :], in_=ps[:, 256:512], func=AF.Silu, reads=[pb], writes=[b_SG])
    for g2 in range(2):
        P.op("dve", "tensor_reduce", out=kmf[:, g2, :], in_=K[:, g2, :].rearrange("p (b k) -> p b k", k=256), axis=AX.X, op=ALU.add,
             reads=[b_K], writes=[b_kmf])
    P.op("dve", "tensor_scalar_mul", out=kmb[:, :, :], in0=kmf[:, :, :], scalar1=1.0 / 256.0, reads=[b_kmf], writes=[b_kmb])
    sbank = [0, 1]
    tri = G.c["tri"]
    G.dummy_mm = 3
    for i in range(NT):
        osb, ob = G.osb.next()
        n = i // 2
        qsl = slice(i * 128, (i + 1) * 128)
        for h in range(4):
            g2 = h // 2
            ph = slice((h % 2) * 64, (h % 2) * 64 + 64)
            bias_t, bb = None, None
            if n > 3:
                P.op("pe", "matmul", G.psF[4][:, 0:16], lhsT=Q[ph, g2, qsl], rhs=kmb[ph, g2, :], start=True, stop=True,
                     reads=[b_Q, b_kmb], writes=[G.bF[4]])
                gsb, gb = gsb_r.next()
                P.op("pool", "memset", gsb[:, :], -1e30, writes=[gb])
                P.op("dve", "tensor_copy", out=gsb[:, 0:n], in_=G.psF[4][:, 0:n], reads=[G.bF[4]], writes=[gb])
                top, tb = top_r.next()
                P.op("dve", "max", out=top[:, :], in_=gsb[:, :], reads=[gb], writes=[tb])
                bias_t, bb = bias_r.next()
                P.op("dve", "tensor_scalar", out=bias_t[:, :], in0=gsb[:, :], scalar1=top[:, 2:3], scalar2=NEG, op0=ALU.is_lt, op1=ALU.mult,
                     reads=[gb, tb], writes=[bb])
            nkt = i + 1

            def addmask(c0, c1, nkt=nkt):
                if c1 == nkt:
                    return [((c1 - c0 - 1) * 128, 128, tri[:, :], [])]
                return []

            def bias_fn(kb0, bias_t=bias_t, n=n):
                j = kb0 // 2
                if bias_t is not None and j < n:
                    return bias_t[:, j:j + 1]
                return None
            def epi(op_, opb, den, db, osb=osb, ob=ob, h=h, i=i):
                P.op("dve", "reciprocal", out=den[:, 1:2], in_=den[:, 0:1], reads=[db], writes=[db])
                P.op("dve", "scalar_tensor_tensor", out=osb[:, h * 64:(h + 1) * 64], in0=op_[:, 0:64], scalar=den[:, 1:2],
                     in1=SG[:, i, h * 64:(h + 1) * 64], op0=ALU.mult, op1=ALU.mult, reads=[opb, db, b_SG], writes=[ob])
                if h == 3:
                    P.dma(O.ap(i, colA), osb[:, 0:256], reads=[ob], writes=[O.buf(i)])
            attn_qtile(
                P, G, [(Q[ph, g2, qsl], lambda c, w, g2=g2, ph=ph: K[ph, g2, c:c + w])], [b_Q, b_K], 0, nkt, addmask,
                bias_fn, [bb] if bb is not None else [], lambda kt, h=h: V[:, kt, h * 64:(h + 1) * 64], [b_V], 64, sbank, 2 + (h % 2), 0.125,
                blk=2, epi=epi)
    G.pipe.flush()
    load_w(P, G, [(1024, 512)])
    for g2 in range(2):
        fm_proj_rope(P, G, g2 * 128, 256 + g2 * 128, 128, lambda tc, g2=g2: Q[:, g2, tc * 512:(tc + 1) * 512], b_Q, "cos2", "sin2", 0)
    load_w(P, G, [(1536, 256)])
    fm_proj_rope(P, G, 0, 128, 128, lambda tc: K[:, 0, tc * 512:(tc + 1) * 512], b_K, "cos2", "sin2", 0)
    load_w(P, G, [(2048, 64), (2368, 256)])
    for ti in range(NT):
        ps, pb = tm_proj(P, G, G.wbf, G.b_wbf, 0, 320, ti, ti % 4)
        P.op("dve", "tensor_copy", out=V[:, ti, 0:64], in_=ps[:, 0:64], reads=[pb], writes=[b_V])
        P.op("act", "activation", out=SG[:, ti, :], in_=ps[:, 64:320], func=AF.Silu, reads=[pb], writes=[b_SG])
    band = G.c["band"]
    for i in range(NT):
        osb, ob = G.osb.next()
        qsl = slice(i * 128, (i + 1) * 128)
        k0 = max(i - 1, 0)
        for h in range(4):
            g2 = h // 2
            ph = slice((h % 2) * 64, (h % 2) * 64 + 64)

            def addmask(c0, c1):
                w = (c1 - c0) * 128
                return [(0, w, band[:, 256 - w:256], [])]
            def epi(op_, opb, den, db, osb=osb, ob=ob, h=h, i=i):
                P.op("dve", "tensor_tensor", out=den[:, 2:3], in0=den[:, 0:1], in1=esink[:, h:h + 1], op=ALU.add, reads=[db, b_esink], writes=[db])
                P.op("dve", "reciprocal", out=den[:, 1:2], in_=den[:, 2:3], reads=[db], writes=[db])
                P.op("dve", "scalar_tensor_tensor", out=osb[:, 256 + h * 64:256 + (h + 1) * 64], in0=op_[:, 0:64], scalar=den[:, 1:2],
                     in1=SG[:, i, h * 64:(h + 1) * 64], op0=ALU.mult, op1=ALU.mult, reads=[opb, db, b_SG], writes=[ob])
                if h == 3:
                    P.dma(O.ap(i, colB), osb[:, 256:512], reads=[ob], writes=[O.buf(i)])
                    O.done(i)
            attn_qtile(
                P, G, [(Q[ph, g2, qsl], lambda c, w, ph=ph: K[ph, 0, c:c + w])], [b_Q, b_K], k0, i + 1, addmask,
                None, [], lambda kt: V[:, kt, 0:64], [b_V], 64, sbank, 2 + (h % 2), 0.125, epi=epi)
    G.pipe.flush()
    G.dummy_mm = 0
    G.dr = dr_save


def phase_sb(P, G, SL, drv, O, b_O, g, col=None):
    col = 512 + 256 * g if col is None else col
    _new_phase(P, SL, G)
    dr_save = G.dr
    G.dr = drv
    cx = SL
    Q = cx.sb([128, 2, S], BF16, "Q"); b_Q = Buf()
    K = cx.sb([128, 2, S], BF16, "K"); b_K = Buf()
    V = cx.sb([128, NT, 256], BF16, "V"); b_V = Buf()
    SG = cx.sb([128, NT, 256], BF16, "SG"); b_SG = Buf()
    Df = cx.sb([128, S + 64], F32, "Df"); b_D = Buf()
    te_r = Rot([cx.sb([128, 512], F32, f"te{i}") for i in range(2)])
    ts_r = Rot([cx.sb([128, 512], F32, f"tsp{i}") for i in range(2)])
    negD_r = Rot([cx.sb([128, 2], F32, f"negD{i}") for i in range(2)])
    deepen_wst(G, cx, 3)
    load_w(P, G, [(0, 512)])
    for g2 in range(2):
        fm_plain_to(P, G, g2 * 128, lambda tc, g2=g2: Q[:, g2, tc * 512:(tc + 1) * 512], b_Q)
    for g2 in range(2):
        fm_plain_to(P, G, 256 + g2 * 128, lambda tc, g2=g2: K[:, g2, tc * 512:(tc + 1) * 512], b_K)
    load_w(P, G, [(512, 512)])
    for ti in range(NT):
        ps, pb = tm_proj(P, G, G.wbf, G.b_wbf, 0, 512, ti, ti % 4)
        P.op("dve", "tensor_copy", out=V[:, ti, :], in_=ps[:, 0:256], reads=[pb], writes=[b_V])
        P.op("act", "activation", out=SG[:, ti, :], in_=ps[:, 256:512], func=AF.Silu, reads=[pb], writes=[b_SG])
    tris = G.c["tris"]
    zeros = G.c["zeros"]
    sb_i = 0
    for i in range(NT):
        osb, ob = G.osb.next()
        qsl = slice(i * 128, (i + 1) * 128)
        nkt = i + 1
        for h in range(4):
            g2 = h // 2
            ph = slice((h % 2) * 64, (h % 2) * 64 + 64)
            P.op("pool", "memset", Df[:, 0:1], 0.0, writes=[b_D])
            for c0 in range(0, nkt, 4):
                c1 = min(c0 + 4, nkt)
                w = (c1 - c0) * 128
                bk = sb_i % 2
                sb_i += 1
                sp_, spb = G.psF[bk], G.bF[bk]
                diag = (c1 == nkt)
                P.op("pe", "matmul", sp_[:, 0:w], lhsT=Q[ph, g2, qsl], rhs=K[ph, g2, c0 * 128:c0 * 128 + w], start=True, stop=not diag,
                     reads=[b_Q, b_K], writes=[spb])
                if diag:
                    P.op("pe", "matmul", sp_[:, w - 128:w], lhsT=G.c["ident_b"][:, :], rhs=tris[:, :], start=False, stop=True,
                         reads=[G.bC], writes=[spb])
                te, teb = te_r.next()
                tsp, tsb = ts_r.next()
                P.op("act", "activation", out=te[:, 0:w], in_=sp_[:, 0:w], func=AF.Exp, scale=0.125, reads=[spb], writes=[teb])
                P.op("act", "activation", out=tsp[:, 0:w], in_=te[:, 0:w], func=AF.Ln, bias=1.0, reads=[teb], writes=[tsb])
                P.op("dve", "tensor_tensor_scan", out=Df[:, c0 * 128 + 1:c0 * 128 + 1 + w], data0=tsp[:, 0:w], data1=zeros[:, 0:w],
                     initial=Df[:, c0 * 128:c0 * 128 + 1], op0=ALU.add, op1=ALU.add, reads=[tsb, b_D, G.bC], writes=[b_D])
                P.op("dve", "scalar_tensor_tensor", out=Df[:, c0 * 128:c0 * 128 + w], in0=sp_[:, 0:w], scalar=0.125,
                     in1=Df[:, c0 * 128:c0 * 128 + w], op0=ALU.mult, op1=ALU.add, reads=[spb, b_D], writes=[b_D])
            negD, nb_ = negD_r.next()
            P.op("dve", "tensor_scalar_mul", out=negD[:, 0:1], in0=Df[:, nkt * 128:nkt * 128 + 1], scalar1=-1.0, reads=[b_D], writes=[nb_])
            op_, opb = G.psF[2 + (h % 2)], G.bF[2 + (h % 2)]
            chunks = list(range(0, nkt, 4))
            for ci, c0 in enumerate(chunks):
                c1 = min(c0 + 4, nkt)
                w = (c1 - c0) * 128
                pch, pcb = G.p_sb.next()
                bi = G.ptb_i[0] % 2
                G.ptb_i[0] += 1
                ptp, ptb = G.psB[bi], G.bB[bi]
                pts, ptsb = G.pt_sb.next()
                last = (ci == len(chunks) - 1)

                def stA(c0=c0, w=w, pch=pch, pcb=pcb, negD=negD, nb_=nb_):
                    P.op("act", "activation", out=pch[:, 0:w], in_=Df[:, c0 * 128:c0 * 128 + w], func=AF.Exp, bias=negD[:, 0:1],
                         reads=[b_D, nb_], writes=[pcb])

                def stB(c0=c0, c1=c1, w=w, pch=pch, pcb=pcb, ptp=ptp, ptb=ptb, pts=pts, ptsb=ptsb):
                    for kk in range(c1 - c0):
                        P.op("pe", "transpose", ptp[:, kk * 128:(kk + 1) * 128], pch[:, kk * 128:(kk + 1) * 128], G.c["ident_b"][:, :],
                             reads=[pcb, G.bC], writes=[ptb])
                    P.op("act", "activation", out=pts[:, 0:w], in_=ptp[:, 0:w], func=AF.Identity, reads=[ptb], writes=[ptsb])

                def stC(c0=c0, c1=c1, pts=pts, ptsb=ptsb, last=last, op_=op_, opb=opb, osb=osb, ob=ob, h=h, i=i, nkt=nkt):
                    for kk in range(c1 - c0):
                        kt = c0 + kk
                        P.op("pe", "matmul", op_[:, 0:64], lhsT=pts[:, kk * 128:(kk + 1) * 128], rhs=V[:, kt, h * 64:(h + 1) * 64],
                             start=(kt == 0), stop=(kt == nkt - 1), reads=[ptsb, b_V], writes=[opb])
                    if last:
                        P.op("dve", "tensor_tensor", out=osb[:, h * 64:(h + 1) * 64], in0=op_[:, 0:64], in1=SG[:, i, h * 64:(h + 1) * 64],
                             op=ALU.mult, reads=[opb, b_SG], writes=[ob])
                        if h == 3:
                            P.dma(O.ap(i, col), osb[:, 0:256], reads=[ob], writes=[O.buf(i)])
                G.pipe.push([stA, stB, stC])
    G.pipe.flush()
    G.dr = dr_save


def phase_dsa(P, G, SL, drv, O, b_O, g, SELB, b_SELB, mode, nbis=14, col=None):
    col = 256 * g if col is None else col
    _new_phase(P, SL, G)
    dr_save = G.dr
    G.dr = drv
    cx = SL
    write = (mode in ("write", "solo"))
    CQN = cx.sb([128, 2, S], BF16, "CQN"); b_CQN = Buf()
    CQR = cx.sb([128, S], BF16, "CQR"); b_CQR = Buf()
    CKR = cx.sb([128, S], BF16, "CKR"); b_CKR = Buf()
    CKVT = cx.sb([128, S], BF16, "CKVT"); b_CKVT = Buf()
    CKV = cx.sb([128, NT, 128], BF16, "CKV"); b_CKV = Buf()
    SG = cx.sb([128, NT, 256], BF16, "SG"); b_SG = Buf()
    if write:
        IQ = cx.sb([128, 2, S], BF16, "IQ"); b_IQ = Buf()
        IK = cx.sb([128, S], BF16, "IK"); b_IK = Buf()
        IW = cx.sb([128, NT, 8], F32, "IW"); b_IW = Buf()
        te_r = G.xs
    else:
        selb_t = [cx.sb([128, S], BF16, f"selb{i}") for i in range(2)]
    st_r = Rot([cx.sb([128, 4], F32, f"rst{i}") for i in range(2)])
    junk = cx.sb([128, 128], F32, "junk"); b_junk = Buf()
    wukT_b = cx.sb([128, 2, 128], BF16, "wukT_b"); wuv_b = cx.sb([128, 4, 64], BF16, "wuv_b"); b_wu = Buf()
    ql_r = Rot([cx.sb([128, 128], BF16, f"ql{i}") for i in range(2)])
    ol_r = Rot([cx.sb([128, 128], BF16, f"ol{i}") for i in range(2)])
    olT_r = Rot([cx.sb([128, 128], BF16, f"olT{i}") for i in range(2)])
    gbc, trif = G.c["gbc"], G.c["trif"]
    st, sb_ = G.wst.next()
    P.dma(st[:, 0:256], drv["wukT"][:, :, :].rearrange("p a b -> p (a b)"), writes=[sb_])
    P.dma(st[:, 256:512], drv["wuv"][:, :, :].rearrange("p a b -> p (a b)"), writes=[sb_])
    P.op("pool", "tensor_copy", out=wukT_b[:, :, :].rearrange("p a b -> p (a b)"), in_=st[:, 0:256], reads=[sb_], writes=[b_wu])
    P.op("pool", "tensor_copy", out=wuv_b[:, :, :].rearrange("p a b -> p (a b)"), in_=st[:, 256:512], reads=[sb_], writes=[b_wu])
    load_w(P, G, [(1024, 256)])
    for g2 in range(2):
        fm_plain_to(P, G, g2 * 128, lambda tc, g2=g2: CQN[:, g2, tc * 512:(tc + 1) * 512], b_CQN)
    load_w(P, G, [(1280, 512)])
    fm_proj_rope(P, G, 0, 128, 128, lambda tc: CQR[:, tc * 512:(tc + 1) * 512], b_CQR, "cos32", "sin32", 0)
    fm_proj_rope(P, G, 256, 384, 128, lambda tc: CKR[:, tc * 512:(tc + 1) * 512], b_CKR, "cos32", "sin32", 0)
    if write:
        load_w(P, G, [(1792, 512)])
        for g2 in range(2):
            fm_proj_rope(P, G, g2 * 128, 256 + g2 * 128, 128, lambda tc, g2=g2: IQ[:, g2, tc * 512:(tc + 1) * 512], b_IQ, "cos32", "sin32", 0)
        load_w(P, G, [(2304, 256)])
        fm_proj_rope(P, G, 0, 128, 128, lambda tc: IK[:, tc * 512:(tc + 1) * 512], b_IK, "cos32", "sin32", 0)
    load_w(P, G, [(2560, 392)])
    for ti in range(NT):
        ps, pb = tm_proj(P, G, G.wbf, G.b_wbf, 0, 392, ti, ti % 4)
        stt, stb = st_r.next()
        P.op("act", "activation", out=junk[:, :], in_=ps[:, 0:128], func=AF.Square, accum_out=stt[:, 0:1], reads=[pb], writes=[b_junk, stb])
        P.op("dve", "tensor_scalar", out=stt[:, 1:2], in0=stt[:, 0:1], scalar1=1.0 / 128.0, scalar2=LN_EPS, op0=ALU.mult, op1=ALU.add,
             reads=[stb], writes=[stb])
        P.op("act", "activation", out=stt[:, 2:3], in_=stt[:, 1:2], func=AF.Sqrt, reads=[stb], writes=[stb])
        P.op("dve", "reciprocal", out=stt[:, 3:4], in_=stt[:, 2:3], reads=[stb], writes=[stb])
        P.op("dve", "scalar_tensor_tensor", out=CKV[:, ti, :], in0=ps[:, 0:128], scalar=stt[:, 3:4], in1=gbc[:, :], op0=ALU.mult, op1=ALU.mult,
             reads=[pb, stb, G.bC], writes=[b_CKV])
        if write:
            P.op("dve", "tensor_copy", out=IW[:, ti, :], in_=ps[:, 128:136], reads=[pb], writes=[b_IW])
        P.op("act", "activation", out=SG[:, ti, :], in_=ps[:, 136:392], func=AF.Silu, reads=[pb], writes=[b_SG])
        bi = ti % 2
        P.op("pe", "transpose", G.psB[bi][:, 0:128], CKV[:, ti, :], G.c["ident_b"][:, :], reads=[b_CKV, G.bC], writes=[G.bB[bi]])
        P.op("dve", "tensor_copy", out=CKVT[:, ti * 128:(ti + 1) * 128], in_=G.psB[bi][:, 0:128], reads=[G.bB[bi]], writes=[b_CKVT])

    carved = []

    def carve():
        nb = Buf()
        for src in (G.b_hT.r, G.b_hT.w):
            for k_, t_ in src.items():
                if k_ not in nb.r or t_[1:] >= nb.r[k_][1:]:
                    nb.r[k_] = t_
        carved.append(nb)
        return nb
    if write:
        selb_t = [G.hT[:, 0, :], G.hT[:, 4, :]]
        cj_t = [G.hT[:, 1, :], G.hT[:, 7, :]]
        sc_t = [G.hT[:, 2:4, :].rearrange("p a s -> p (a s)").bitcast(F32), G.hT[:, 5:7, :].rearrange("p a s -> p (a s)").bitcast(F32)]
        b_selb = [carve(), carve()]
        b_cjs = [carve(), carve()]
        b_scs = [carve(), carve()]
        ck = cx.sb([128, 32], F32, "ck"); b_ck = Buf()
        for k_ in range(nbis + 1):
            P.op("pool", "memset", ck[:, k_:k_ + 1], 0.5 ** (k_ + 1), writes=[b_ck])
        bs_r = Rot([cx.sb([128, 48], F32, f"bs{i}") for i in range(3)])
    else:
        b_selb = [Buf(), Buf()]
    xc = [0]

    def prologue(i):
        nkt = i + 1
        n = nkt * 128
        qsl = slice(i * 128, (i + 1) * 128)
        selbias, bsel = selb_t[i % 2], b_selb[i % 2]
        if not write:
            P.dma(selbias[:, 0:n], SELB[i * 128:(i + 1) * 128, 0:n], reads=[b_SELB], writes=[bsel])
            yield
            return
        score, b_sc = sc_t[i % 2], b_scs[i % 2]
        cjunk, b_cj = cj_t[i % 2], b_cjs[i % 2]
        for c0 in range(0, nkt, 4):
            c1 = min(c0 + 4, nkt)
            w = (c1 - c0) * 128
            cols = slice(c0 * 128, c0 * 128 + w)
            for hi in range(8):
                q4, grp = hi % 4, hi // 4
                rows = slice(32 * q4, 32 * q4 + 32)
                bk = 4 + (xc[0] % 2)
                xc[0] += 1
                kw = {"tile_position": (96, 0)} if q4 == 3 else {}
                P.op("pe", "matmul", G.psF[bk][:, 0:w], lhsT=IQ[rows, grp, qsl], rhs=IK[rows, cols], start=True, stop=True,
                     reads=[b_IQ, b_IK], writes=[G.bF[bk]], **kw)
                te, teb = te_r.next()
                P.op("act", "activation", out=te[:, 0:w], in_=G.psF[bk][:, 0:w], func=AF.Relu, reads=[G.bF[bk]], writes=[teb])
                if hi == 0:
                    P.op("dve", "tensor_scalar_mul", out=score[:, cols], in0=te[:, 0:w], scalar1=IW[:, i, 0:1], reads=[teb, b_IW], writes=[b_sc])
                else:
                    P.op("dve", "scalar_tensor_tensor", out=score[:, cols], in0=te[:, 0:w], scalar=IW[:, i, hi:hi + 1], in1=score[:, cols],
                         op0=ALU.mult, op1=ALU.add, reads=[teb, b_IW, b_sc], writes=[b_sc])
                yield
        bs, bb = bs_r.next()
        LO, HI, MID, CNT, TT, W0 = (bs[:, j:j + 1] for j in range(6))
        WK = bs[:, 16:48]
        if i >= 2:
            P.op("dve", "tensor_reduce", out=LO, in_=score[:, 0:n], axis=AX.X, op=ALU.min, reads=[b_sc], writes=[bb])
            P.op("dve", "tensor_scalar_add", out=LO, in0=LO, scalar1=-1.0, reads=[bb], writes=[bb])
        else:
            P.op("dve", "memset", LO, -1e29, writes=[bb])
        P.op("dve", "tensor_tensor", out=score[:, n - 128:n], in0=score[:, n - 128:n], in1=trif[:, :], op=ALU.add, reads=[b_sc, G.bC, bb], writes=[b_sc])
        yield
        if i >= 2:
            P.op("dve", "max", out=bs[:, 8:16], in_=score[:, 0:n], reads=[b_sc], writes=[bb])
            P.op("dve", "tensor_tensor", out=W0, in0=bs[:, 8:9], in1=LO, op=ALU.subtract, reads=[bb], writes=[bb])
            P.op("dve", "tensor_scalar_mul", out=WK, in0=ck[:, :], scalar1=W0, reads=[bb, b_ck], writes=[bb])
            P.op("dve", "tensor_tensor", out=MID, in0=LO, in1=WK[:, 0:1], op=ALU.add, reads=[bb], writes=[bb])
            yield
            na = max(128, (n * 9 // 16) // 128 * 128)
            bA, bD, bjD = Buf(), Buf(), Buf()
            for src_ in (bb.w, bb.r):
                for k_, t_ in src_.items():
                    bA.r[k_] = t_; bD.r[k_] = t_
            G2, C2 = bs[:, 6:7], bs[:, 7:8]
            for it in range(nbis):
                P.op("act", "activation", out=cjunk[:, 0:na], in_=score[:, 0:na], func=AF.Sign, scale=-1.0, bias=MID, accum_out=CNT,
                     reads=[b_sc, bb], writes=[bA, b_cj])
                P.op("dve", "tensor_scalar", out=cjunk[:, na:n], in0=score[:, na:n], scalar1=MID, scalar2=0.0, op0=ALU.is_gt, op1=ALU.add,
                     accum_out=G2, reads=[b_sc, bb], writes=[bD, bjD])
                P.op("dve", "scalar_tensor_tensor", out=C2, in0=G2, scalar=-2.0, in1=CNT, op0=ALU.mult, op1=ALU.add,
                     reads=[bA, bD], writes=[bb])
                P.op("dve", "scalar_tensor_tensor", out=TT, in0=C2, scalar=float(na - 511), in1=WK[:, it:it + 1], op0=ALU.is_le, op1=ALU.mult,
                     reads=[bb], writes=[bb])
                P.op("dve", "scalar_tensor_tensor", out=MID, in0=TT, scalar=WK[:, it + 1:it + 2], in1=MID, op0=ALU.subtract, op1=ALU.add,
                     reads=[bb, bA, bD], writes=[bb])
                yield
            P.op("dve", "tensor_tensor", out=LO, in0=MID, in1=WK[:, nbis:nbis + 1], op=ALU.subtract, reads=[bb], writes=[bb])
        P.op("dve", "tensor_scalar", out=selbias[:, 0:n], in0=score[:, 0:n], scalar1=LO, scalar2=NEG * 8.0, op0=ALU.is_le, op1=ALU.mult,
             reads=[b_sc, bb], writes=[bsel])
        if mode == "write":
            P.dma(SELB[i * 128:(i + 1) * 128, 0:n], selbias[:, 0:n], reads=[bsel], writes=[b_SELB])
        yield

    def nsteps(i):
        if not write:
            return 1
        return 8 * ((i + 4) // 4) + 3 + (nbis if i >= 2 else 0)

    sbank = [0, 1]
    G.evac_eng = "act"
    DEPTH = 2 if write else 1
    gens = {}
    for j in range(min(DEPTH, NT)):
        gens[j] = prologue(j)
    for _ in gens[0]:
        pass
    for i in range(NT):
        osb, ob = G.osb.next()
        qsl = slice(i * 128, (i + 1) * 128)
        nkt = i + 1
        selbias, bsel = selb_t[i % 2], b_selb[i % 2]
        if i + DEPTH < NT and (i + DEPTH) not in gens:
            gens[i + DEPTH] = prologue(i + DEPTH)
        active = [(j, gens[j]) for j in range(i + 1, i + DEPTH + 1) if j in gens]
        npush = 4 * ((nkt + 3) // 4)
        pers = [(gj, (nsteps(j) + npush * DEPTH - 1) // (npush * DEPTH) if j > i + 1 else (nsteps(j) + npush - 1) // npush) for j, gj in active]

        def after_push(pers=pers):
            for gj, per in pers:
                for _ in range(per):
                    next(gj, None)
        G.after_push = after_push
        for h in range(4):
            g2 = h // 2
            ph = slice((h % 2) * 64, (h % 2) * 64 + 64)
            rows = slice(32 * h, 32 * h + 32)
            kw = {"tile_position": (96, 0)} if h == 3 else {}
            ql, qlb = ql_r.next()
            P.op("pe", "matmul", G.psF[4][:, 0:128], lhsT=wukT_b[ph, g2, :], rhs=CQN[ph, g2, qsl], start=True, stop=True,
                 reads=[b_wu, b_CQN], writes=[G.bF[4]])
            P.op("act", "activation", out=ql[:, :], in_=G.psF[4][:, 0:128], func=AF.Identity, reads=[G.bF[4]], writes=[qlb])

            def addmask(c0, c1, selbias=selbias, bsel=bsel):
                w = (c1 - c0) * 128
                return [(0, w, selbias[:, c0 * 128:c0 * 128 + w], [bsel])]

            def epi(op_, opb, den, db, osb=osb, ob=ob, h=h, i=i):
                P.op("dve", "reciprocal", out=den[:, 1:2], in_=den[:, 0:1], reads=[db], writes=[db])
                ol, olb = ol_r.next()
                P.op("dve", "tensor_scalar_mul", out=ol[:, :], in0=op_[:, 0:128], scalar1=den[:, 1:2], reads=[opb, db], writes=[olb])
                P.op("pe", "transpose", G.psB[0][:, 0:128], ol[:, :], G.c["ident_b"][:, :], reads=[olb, G.bC], writes=[G.bB[0]])
                olT, oltb = olT_r.next()
                P.op("act", "activation", out=olT[:, :], in_=G.psB[0][:, 0:128], func=AF.Identity, reads=[G.bB[0]], writes=[oltb])
                P.op("pe", "matmul", G.psF[5][:, 0:64], lhsT=olT[:, :], rhs=wuv_b[:, h, :], start=True, stop=True, reads=[oltb, b_wu], writes=[G.bF[5]])
                P.op("dve", "tensor_tensor", out=osb[:, h * 64:(h + 1) * 64], in0=G.psF[5][:, 0:64], in1=SG[:, i, h * 64:(h + 1) * 64], op=ALU.mult,
                     reads=[G.bF[5], b_SG], writes=[ob])
                if h == 3:
                    P.dma(O.ap(i, col), osb[:, 0:256], reads=[ob], writes=[O.buf(i)])
                    O.done(i)
            attn_qtile(
                P, G, [(ql[:, :], lambda c, w: CKVT[:, c:c + w]), (CQR[rows, qsl], lambda c, w, rows=rows: CKR[rows, c:c + w], kw)],
                [qlb, b_CKVT, b_CQR, b_CKR], 0, nkt, addmask, None, [], lambda kt: CKV[:, kt, :], [b_CKV], 128, sbank, 2 + (h % 2), DSA_SCALE, epi=epi)
        G.after_push = None
        if (i + 1) in gens:
            for _ in gens[i + 1]:
                pass
    G.pipe.flush()
    G.evac_eng = "dve"
    for cb in carved:
        for src in (cb.r, cb.w):
            for k_, t_ in src.items():
                if k_ not in G.b_hT.r or t_[1:] >= G.b_hT.r[k_][1:]:
                    G.b_hT.r[k_] = t_
    G.dr = dr_save


def phase_out(P, G, SL, drv, x_src, x_reads, O, b_O, dst, b_dst, finals=None, gathered=False):
    _new_phase(P, SL, G)
    cx = SL
    psF, bF = G.psF, G.bF
    rows = cx.sb([1, 3, D], F32, "rows"); b_rows = Buf()
    P.dma(rows[0:1, 0, :], G.grow[0:1, :], reads=[G.b_grow], writes=[b_rows])
    P.dma(rows[0:1, 1, :], drv["lng"][:, :], writes=[b_rows])
    P.dma(rows[0:1, 2, :], drv["lnb"][:, :], writes=[b_rows])
    bc = cx.sb([128, 3, D], F32, "bc"); b_bc = Buf()
    ones_row = G.c["ones_row"]
    for j in range(3):
        for hh in range(2):
            bk = j * 2 + hh
            P.op("pe", "matmul", psF[bk][:, :], lhsT=ones_row[0:1, :], rhs=rows[0:1, j, hh * 512:(hh + 1) * 512], start=True, stop=True,
                 reads=[G.bC, b_rows], writes=[bF[bk]])
            P.op("dve", "tensor_copy", out=bc[:, j, hh * 512:(hh + 1) * 512], in_=psF[bk][:, :], reads=[bF[bk]], writes=[b_bc])
    wbf = cx.sb([128, 8, D], BF16, "wbf_out"); b_wbf = Buf()
    for k in range(8):
        for hh in range(2):
            st, sb_ = G.wst.next()
            P.dma(st[:, :], drv["w_out"][k * 128:(k + 1) * 128, hh * 512:(hh + 1) * 512], writes=[sb_])
            P.op("pool", "tensor_copy", out=wbf[:, k, hh * 512:(hh + 1) * 512], in_=st[:, :], reads=[sb_], writes=[b_wbf])
    os_ = Rot([cx.sb([128, D], F32, f"os{i}") for i in range(2)])
    oT = Rot([cx.sb([128, 8, 128], BF16, f"oT{i}") for i in range(2)])
    zs = Rot([cx.sb([128, D], F32, f"zs{i}") for i in range(2)])
    ys = Rot([cx.sb([128, D], F32, f"ys{i}") for i in range(2)])
    st_r = Rot([cx.sb([128, 8], F32, f"st{i}") for i in range(2)])
    junk = cx.sb([128, D], F32, "junk"); b_junk = Buf()
    ident = G.c["ident_f"]
    def loads(ti):
        xt, xb = G.xs.next()
        ot, otb = os_.next()
        P.dma(xt[:, :], x_src[ti * 128:(ti + 1) * 128, :], reads=list(x_reads), writes=[xb])
        if gathered:
            for q_, (rk, c0) in enumerate(((0, 0), (1, 0), (0, 256), (1, 256))):
                P.dma(ot[:, q_ * 256:(q_ + 1) * 256], O.gathered_ap(ti, rk, c0), reads=[O.gbuf(ti)], writes=[otb])
        else:
            P.dma(ot[:, :], O[ti * 128:(ti + 1) * 128, :], reads=[b_O], writes=[otb])
        return xt, xb, ot, otb
    nxt_ld = loads(0)
    for ti in range(NT):
        xt, xb, ot, otb = nxt_ld
        if ti + 1 < NT:
            nxt_ld = loads(ti + 1)
        oTt, oTb = oT.next()
        for half in range(2):
            bk = (ti % 2) * 2 + half
            for j in range(4):
                k = half * 4 + j
                P.op("pe", "transpose", psF[bk][:, j * 128:(j + 1) * 128], ot[:, k * 128:(k + 1) * 128], ident[:, :], reads=[otb, G.bC], writes=[bF[bk]])
            if half:
                P.op("act", "activation", out=oTt[:, 4:8, :], in_=psF[bk][:, :].rearrange("p (a b) -> p a b", b=128), func=AF.Identity,
                     reads=[bF[bk]], writes=[oTb])
            else:
                P.op("dve", "tensor_copy", out=oTt[:, 0:4, :], in_=psF[bk][:, :].rearrange("p (a b) -> p a b", b=128), reads=[bF[bk]], writes=[oTb])
        zt, zb = zs.next()
        for hh in range(2):
            bk = 4 + hh
            for k in range(8):
                P.op("pe", "matmul", psF[bk][:, :], lhsT=oTt[:, k, :], rhs=wbf[:, k, hh * 512:(hh + 1) * 512], start=(k == 0), stop=(k == 7),
                     reads=[oTb, b_wbf], writes=[bF[bk]])
            P.op("dve", "tensor_tensor", out=zt[:, hh * 512:(hh + 1) * 512], in0=psF[bk][:, :], in1=bc[:, 0, hh * 512:(hh + 1) * 512], op=ALU.mult,
                 reads=[bF[bk], b_bc], writes=[zb])
        P.op("dve", "scalar_tensor_tensor", out=zt[:, :], in0=xt[:, :], scalar=ALPHA, in1=zt[:, :], op0=ALU.mult, op1=ALU.add,
             reads=[xb, zb], writes=[zb])
        stt, stb = st_r.next()
        P.op("dve", "reduce_sum", out=stt[:, 0:1], in_=zt[:, :], axis=AX.X, reads=[zb], writes=[stb])
        P.op("act", "activation", out=junk[:, :], in_=zt[:, :], func=AF.Square, accum_out=stt[:, 1:2], reads=[zb], writes=[b_junk, stb])
        P.op("dve", "tensor_scalar_mul", out=stt[:, 2:3], in0=stt[:, 0:1], scalar1=1.0 / D, reads=[stb], writes=[stb])
        P.op("dve", "tensor_tensor", out=stt[:, 3:4], in0=stt[:, 2:3], in1=stt[:, 2:3], op=ALU.mult, reads=[stb], writes=[stb])
        P.op("dve", "scalar_tensor_tensor", out=stt[:, 4:5], in0=stt[:, 1:2], scalar=1.0 / D, in1=stt[:, 3:4], op0=ALU.mult, op1=ALU.subtract,
             reads=[stb], writes=[stb])
        P.op("dve", "tensor_scalar_add", out=stt[:, 4:5], in0=stt[:, 4:5], scalar1=LN_EPS, reads=[stb], writes=[stb])
        P.op("act", "activation", out=stt[:, 5:6], in_=stt[:, 4:5], func=AF.Sqrt, reads=[stb], writes=[stb])
        P.op("dve", "reciprocal", out=stt[:, 6:7], in_=stt[:, 5:6], reads=[stb], writes=[stb])
        yt, yb = ys.next()
        P.op("dve", "tensor_scalar", out=yt[:, :], in0=zt[:, :], scalar1=stt[:, 2:3], scalar2=stt[:, 6:7], op0=ALU.subtract, op1=ALU.mult,
             reads=[zb, stb], writes=[yb])
        P.op("pool", "tensor_tensor", out=yt[:, :], in0=yt[:, :], in1=bc[:, 1, :], op=ALU.mult, reads=[yb, b_bc], writes=[yb])
        P.op("pool", "tensor_tensor", out=yt[:, :], in0=yt[:, :], in1=bc[:, 2, :], op=ALU.add, reads=[yb, b_bc], writes=[yb])
        tok = P.dma(dst[ti * 128:(ti + 1) * 128, :], yt[:, :], reads=[yb], writes=[b_dst] if b_dst is not None else [])
        if finals is not None:
            finals.append(tok)


PAIRS = [[0, 1], [2, 3], [4, 5], [6, 7]]
CH_ROWS = 1024
NCH = S // CH_ROWS
TPC = CH_ROWS // 128


class ChunkedO:
    def __init__(self, P, og_tensors, oa_tensors):
        self.P = P
        self.og_t, self.oa_t = og_tensors, oa_tensors
        self.og = [t.ap() for t in og_tensors]
        self.oa = [t.ap() for t in oa_tensors]
        self.b_og = [Buf() for _ in og_tensors]
        self.b_oa = [Buf() for _ in oa_tensors]
        self.sent = [False] * len(og_tensors)

    def ap(self, i, c0):
        k, r = i // TPC, (i % TPC) * 128
        return self.og[k][r:r + 128, c0:c0 + 256]

    def buf(self, i):
        return self.b_og[i // TPC]

    def _send(self, k):
        if not self.sent[k]:
            self.sent[k] = True
            self.P.coll("AllGather", PAIRS, self.og_t[k].ap().opt(), self.oa_t[k].ap().opt(), reads=[self.b_og[k]], writes=[self.b_oa[k]])

    def done(self, i):
        if self.armed and i % TPC == TPC - 1:
            self._send(i // TPC)

    armed = False

    def flush(self):
        for k in range(len(self.og)):
            self._send(k)

    def gathered_ap(self, ti, rank, c0):
        k, r = ti // TPC, (ti % TPC) * 128
        return self.oa[k][rank * CH_ROWS + r:rank * CH_ROWS + r + 128, c0:c0 + 256]

    def gbuf(self, ti):
        return self.b_oa[ti // TPC]


def build_fused2():
    nc = bass.Bass("TRN2", target_bir_lowering=False)
    dr = {}
    for nm, shp in (("x", [S, D]), ("cvec", [128, 8]), ("w_ada", [2, D, 3072]), ("b_ada", [2, 1, 3072]),
                    ("w_in_even", [D, EV_COLS]), ("w_in_odd", [D, OD_COLS]), ("w_out", [2, D, D]), ("lng", [2, 1, D]), ("lnb", [2, 1, D]),
                    ("sinks", [128, 4]), ("wukT", [128, 2, 128]), ("wuv", [128, 4, 64]),
                    ("cos2", [128, S]), ("sin2", [128, S]), ("cos32", [128, S]), ("sin32", [128, S]),
                    ("ident_f", [128, 128]), ("one", [65, 1]), ("ones_row", [1, 128]), ("zeros", [128, 512]), ("trif", [128, 128]), ("gbc", [128, 128])):
        _din(nc, dr, nm, shp)
    for nm, shp in (("ident_b", [128, 128]), ("tri", [128, 128]), ("tris", [128, 128]), ("band", [128, 256])):
        _din(nc, dr, nm, shp, BF16)
    y_out = nc.dram_tensor("y", [S, D], F32, kind="ExternalOutput").ap()
    X1 = nc.dram_tensor("x1_scratch", [S, D], F32, kind="Internal").ap()
    OGt = [[nc.dram_tensor(f"og{l}_{k}", [CH_ROWS, 512], F32) for k in range(NCH)] for l in range(2)]
    OAt = [[nc.dram_tensor(f"oa{l}_{k}", [2 * CH_ROWS, 512], F32) for k in range(NCH)] for l in range(2)]
    P = Prog(nc)
    Buf.default_r = {}
    finals = []
    with ExitStack() as es:
        extra = [("band", [128, 256], BF16), ("tris", [128, 128], BF16), ("zeros", [128, 512], F32), ("trif", [128, 128], F32),
                 ("gbc", [128, 128], F32), ("ones_row", [1, 128], F32)]
        G = setup_common(nc, P, es, dr, extra, wbf_cols=512)
        big = G.cx.sb([128, SLAB_ELEMS], BF16, "slab")
        SL = SlabCtx(big, SLAB_ELEMS)
        G.lcx = SL
        G.modc_t = G.cx.sb([128, 16], F32, "modc_p"); G.b_modc = Buf()
        G.grow = G.cx.sb([1, D], F32, "grow_p"); G.b_grow = Buf()
        b_X1 = Buf()
        for layer in range(2):
            x_src, x_reads = (dr["x"], []) if layer == 0 else (X1, [b_X1])
            OG = ChunkedO(P, OGt[layer], OAt[layer])
            b_OG = None
            _new_phase(P, SL, G)
            G.dr = {"cvec": dr["cvec"], "w_ada": dr["w_ada"][layer], "b_ada": dr["b_ada"][layer]}
            emit_adaln(P, G)
            P.dma(G.grow[0:1, :], G.modrow[64:65, :], reads=[G.b_modrow], writes=[G.b_grow])
            emit_hT(P, G, x_src, x_reads)
            if layer == 0:
                drv = {"w_in": dr["w_in_even"], "sinks": dr["sinks"], "cos2": dr["cos2"], "sin2": dr["sin2"]}
                OG.armed = True
                phase_even(P, G, SL, drv, OG, b_OG, 0, colA=0, colB=256)
            else:
                drv = {"w_in": dr["w_in_odd"], "cos32": dr["cos32"], "sin32": dr["sin32"], "wukT": dr["wukT"], "wuv": dr["wuv"]}
                phase_sb(P, G, SL, drv, OG, b_OG, 0, col=256)
                OG.armed = True
                phase_dsa(P, G, SL, drv, OG, b_OG, 0, None, None, "solo", col=0)
            OG.flush()
            OA, b_OA = OG, None
            drv = {"w_out": dr["w_out"][layer], "lng": dr["lng"][layer], "lnb": dr["lnb"][layer]}
            if layer == 0:
                phase_out(P, G, SL, drv, x_src, x_reads, OA, b_OA, X1, b_X1, gathered=True)
            else:
                phase_out(P, G, SL, drv, x_src, x_reads, OA, b_OA, y_out, None, finals, gathered=True)
        P.emit(final_waits=finals)
    Buf.default_r = {}
    return nc


def _fused2_inputs(x, c, w_ada, b_ada, w_in_even, sinks, w_in_odd, kv_g, w_uk, w_uv, w_out, ln_g, ln_b):
    ev = _even_inputs(x, c, w_ada[0], b_ada[0], w_in_even, sinks)
    od = _odd_inputs(x, c, w_ada[1], b_ada[1], w_in_odd, kv_g, w_uk, w_uv)
    shared = {
        "w_ada": np.ascontiguousarray(w_ada), "b_ada": np.ascontiguousarray(b_ada.reshape(2, 1, 3072)),
        "w_out": np.ascontiguousarray(w_out), "lng": np.ascontiguousarray(ln_g.reshape(2, 1, D)), "lnb": np.ascontiguousarray(ln_b.reshape(2, 1, D)),
        "ones_row": np.ones((1, 128), np.float32),
    }
    for k in ("cos2", "sin2", "ident_f", "one", "ident_b", "tri", "band"):
        shared[k] = ev[0][k]
    for k in ("cos32", "sin32", "tris", "zeros", "trif", "gbc"):
        shared[k] = od[0][k]
    maps = []
    for core in range(8):
        m = dict(shared)
        m["x"] = ev[core]["x"]
        m["cvec"] = ev[core]["cvec"]
        m["w_in_even"] = ev[core]["w_in"]
        m["sinks"] = ev[core]["sinks"]
        m["w_in_odd"] = od[core]["w_in"]
        m["wukT"] = od[core]["wukT"]
        m["wuv"] = od[core]["wuv"]
        maps.append(m)
    return maps


def kernel(x, c, w_ada, b_ada, w_in_even, sink_logits, w_in_odd, kv_norm_g, w_uk, w_uv, w_out, ln_g, ln_b):
    f = lambda a: np.asarray(a, np.float32)
    maps = _fused2_inputs(f(x), f(c), f(w_ada), f(b_ada), f(w_in_even[0]), f(sink_logits[0]), f(w_in_odd[0]), f(kv_norm_g[0]),
                          f(w_uk[0]), f(w_uv[0]), f(w_out), f(ln_g), f(ln_b))
    res = _run(_get_nc("fused2", build_fused2), maps)
    return np.stack([res[2 * b]["y"] for b in range(4)]).astype(np.float32)
```
